# Optimizing a Trainium2 kernel written in Bass

```python
import math
import jax, jax.numpy as jnp
from jax import lax
import numpy as np

D_MODEL = 1024
BATCH = 16
SEQ = 256
DEPTH = 1
DEC_BATCH = 4
DEC_SEQ = 2048
PAST_LEN = 256

GRID_W = 64
MIX_WIDTH = D_MODEL
RET_WIDTH = MIX_WIDTH // 2
RET_HEADS = 4
RET_DK = RET_WIDTH // RET_HEADS
RET_DV = RET_DK
S5_WIDTH = MIX_WIDTH - RET_WIDTH
S5_CH = 16
S5_GROUPS = S5_WIDTH // S5_CH
S5_STATE = 64
D_FF = 4 * D_MODEL
CTX_CHUNK = 64
N_DIR = 2
ALPHA = (2 * DEPTH) ** 0.25
BETA = (8 * DEPTH) ** -0.25
LN_EPS = 1e-5
IN_COLS = 4 * RET_WIDTH + S5_WIDTH

kernel_name = "hymba_retnet_s5_prefix_dit_step"


def _norm(x):
    xf = x.astype(jnp.float32)
    mu = jnp.mean(xf, axis=-1, keepdims=True)
    var = jnp.mean(jnp.square(xf - mu), axis=-1, keepdims=True)
    return ((xf - mu) * lax.rsqrt(var + LN_EPS)).astype(x.dtype)


def _ln_affine(x, g, b):
    return (_norm(x).astype(jnp.float32) * g.astype(jnp.float32) + b.astype(jnp.float32)).astype(x.dtype)


def _retention_forward(q, k, v, log_gamma, s0, chunk, n_chunks):
    b, h, _, dk = q.shape
    dv = v.shape[-1]
    qc = q.reshape(b, h, n_chunks, chunk, dk)
    kc = k.reshape(b, h, n_chunks, chunk, dk)
    vc = v.reshape(b, h, n_chunks, chunk, dv)
    pos = jnp.arange(chunk, dtype=jnp.float32)
    lg = log_gamma[:, None]
    rel = pos[:, None] - pos[None, :]
    decay_mat = jnp.where(rel[None] >= 0, jnp.exp(lg[..., None] * jnp.maximum(rel, 0.0)[None]), 0.0)
    scores = jnp.einsum('bhnid,bhnjd->bhnij', qc, kc) * decay_mat[None, :, None]
    o_intra = jnp.einsum('bhnij,bhnje->bhnie', scores, vc)
    k_dec = kc * jnp.exp(lg * (chunk - 1 - pos))[None, :, None, :, None]
    kv = jnp.einsum('bhnjd,bhnje->nbhde', k_dec, vc)
    chunk_decay = jnp.exp(log_gamma * chunk)[None, :, None, None]

    def step(s, kv_n):
        return chunk_decay * s + kv_n, s

    s_final, s_prev = lax.scan(step, s0, kv)
    o_inter = jnp.einsum('bhnid,nbhde->bhnie', qc, s_prev) * jnp.exp(lg * (pos + 1.0))[None, :, None, :, None]
    o = (o_intra + o_inter).reshape(b, h, n_chunks * chunk, dv)
    return o, s_final


def _bidir_retention(q, k, v, log_gamma2, s0_2, chunk, n_chunks):
    flip = lambda t: jnp.flip(t, axis=2)
    o_f, s_f = _retention_forward(q, k, v, log_gamma2[0], s0_2[:, 0], chunk, n_chunks)
    o_b, s_b = _retention_forward(flip(q), flip(k), flip(v), log_gamma2[1], s0_2[:, 1], chunk, n_chunks)
    return o_f + flip(o_b), jnp.stack([s_f, s_b], axis=1)


def _lin_combine(e1, e2):
    a1, x1 = e1
    a2, x2 = e2
    return a1 * a2, a2 * x1 + x2


def _s5_scan(u_c, a_re, a_im, log_dt, b_mat, s0):
    l = u_c.shape[1]
    a = lax.complex(a_re, a_im)
    a_dt = a * jnp.exp(log_dt)[:, None]
    a_bar = jnp.exp(a_dt)
    b_bar = ((a_bar - 1.0) / a)[..., None] * b_mat
    bu = jnp.einsum('gpc,blgc->blgp', b_bar, u_c)
    a_all = jnp.broadcast_to(a_bar, bu.shape)
    _, s = lax.associative_scan(_lin_combine, (a_all, bu), axis=1)
    steps = jnp.arange(1, l + 1, dtype=jnp.float32)[:, None, None]
    s = s + jnp.exp(steps * a_dt[None])[None] * s0[:, None]
    return s, s[:, -1]


def _bidir_s5(u, a_re2, a_im2, log_dt2, b_re, b_im, c_re, c_im, d_skip, s0_re2, s0_im2):
    f32 = lambda t: t.astype(jnp.float32)
    u_c = u.astype(jnp.complex64)
    b_mat = lax.complex(f32(b_re), f32(b_im))
    c_mat = lax.complex(f32(c_re), f32(c_im))
    s0 = lax.complex(f32(s0_re2), f32(s0_im2))
    s_f, fin_f = _s5_scan(u_c, f32(a_re2[0]), f32(a_im2[0]), f32(log_dt2[0]), b_mat, s0[:, 0])
    s_b, fin_b = _s5_scan(jnp.flip(u_c, axis=1), f32(a_re2[1]), f32(a_im2[1]), f32(log_dt2[1]), b_mat, s0[:, 1])
    s = s_f + jnp.flip(s_b, axis=1)
    y = jnp.real(jnp.einsum('gcp,blgp->blgc', c_mat, s)) + f32(d_skip).reshape(S5_GROUPS, S5_CH) * u
    fin = jnp.stack([fin_f, fin_b], axis=1)
    return y, jnp.real(fin), jnp.imag(fin)


def _trunk_layer(x, cond, ret_s0, s5_s0_re, s5_s0_im, chunk, n_chunks,
                 w_ada, b_ada, w_in, ret_decay, s5_a_re, s5_a_im, s5_log_dt,
                 s5_b_re, s5_b_im, s5_c_re, s5_c_im, s5_d, w_glu, b_glu, w_out,
                 ln1_g, ln1_b, w_ff1, b_ff1, w_ff2, b_ff2, ln2_g, ln2_b):
    bsz, l, _ = x.shape
    mod = jax.nn.silu(cond) @ w_ada + b_ada
    sh1, sc1, g1, sh2, sc2, g2 = jnp.split(mod[:, None, :], 6, axis=-1)
    h = _norm(x) * (1.0 + sc1) + sh1
    proj = h @ w_in
    q, k, v, g, u = jnp.split(proj, [RET_WIDTH, 2 * RET_WIDTH, 3 * RET_WIDTH, 4 * RET_WIDTH], axis=-1)
    heads = lambda t: t.reshape(bsz, l, RET_HEADS, RET_DK).transpose(0, 2, 1, 3).astype(jnp.float32)
    o_ret, ret_state = _bidir_retention(heads(q), heads(k) * (RET_DK ** -0.5), heads(v),
                                        jax.nn.log_sigmoid(ret_decay.astype(jnp.float32)),
                                        ret_s0.astype(jnp.float32), chunk, n_chunks)
    o_ret = _norm(o_ret).transpose(0, 2, 1, 3).reshape(bsz, l, RET_WIDTH)
    o_ret = o_ret * jax.nn.silu(g.astype(jnp.float32))
    y, s5_re, s5_im = _bidir_s5(u.reshape(bsz, l, S5_GROUPS, S5_CH).astype(jnp.float32),
                                s5_a_re, s5_a_im, s5_log_dt, s5_b_re, s5_b_im, s5_c_re, s5_c_im,
                                s5_d, s5_s0_re, s5_s0_im)
    y = jax.nn.gelu(y.reshape(bsz, l, S5_WIDTH))
    y = y * jax.nn.sigmoid(y @ w_glu.astype(jnp.float32) + b_glu.astype(jnp.float32))
    mix = jnp.concatenate([o_ret, y], axis=-1).astype(x.dtype) @ w_out
    x = _ln_affine(ALPHA * x + g1 * mix, ln1_g, ln1_b)
    h = _norm(x) * (1.0 + sc2) + sh2
    f = jnp.square(jax.nn.relu(h @ w_ff1 + b_ff1)) @ w_ff2 + b_ff2
    x = _ln_affine(ALPHA * x + g2 * f, ln2_g, ln2_b)
    return x, ret_state, s5_re, s5_im


def setup_inputs(seed: int = 0) -> dict:
    key = jax.random.key(seed)
    ks = jax.random.split(key, 40)
    nrm = lambda i, shape, s: s * jax.random.normal(ks[i], shape, jnp.float32)
    ret_init = jnp.asarray(np.log(2.0 ** (5 + np.arange(RET_HEADS)) - 1.0), jnp.float32)
    a_im_init = jnp.asarray(np.pi * np.arange(S5_STATE), jnp.float32)
    inp = {}
    inp["x_prompt"] = nrm(0, (BATCH, SEQ, D_MODEL), 1.0)
    inp["x_sample"] = nrm(1, (DEC_BATCH, DEC_SEQ, D_MODEL), 1.0)
    inp["state_ret"] = nrm(2, (DEC_BATCH, DEPTH, N_DIR, RET_HEADS, RET_DK, RET_DV), 1.0)
    inp["state_s5_re"] = nrm(3, (DEC_BATCH, DEPTH, N_DIR, S5_GROUPS, S5_STATE), 0.1)
    inp["state_s5_im"] = nrm(4, (DEC_BATCH, DEPTH, N_DIR, S5_GROUPS, S5_STATE), 0.1)
    inp["c"] = nrm(5, (DEC_BATCH, D_MODEL), 1.0)
    inp["c_ctx"] = nrm(6, (D_MODEL,), 1.0)
    inp["w_ada"] = nrm(7, (DEPTH, D_MODEL, 6 * D_MODEL), 0.5 * D_MODEL ** -0.5)
    inp["b_ada"] = nrm(8, (DEPTH, 6 * D_MODEL), 0.02)
    inp["w_in"] = nrm(9, (DEPTH, D_MODEL, IN_COLS), D_MODEL ** -0.5)
    inp["ret_decay"] = ret_init + nrm(10, (DEPTH, N_DIR, RET_HEADS), 0.05)
    inp["s5_a_re"] = -0.5 + nrm(11, (DEPTH, N_DIR, S5_GROUPS, S5_STATE), 0.01)
    inp["s5_a_im"] = a_im_init + nrm(12, (DEPTH, N_DIR, S5_GROUPS, S5_STATE), 0.01)
    inp["s5_log_dt"] = jax.random.uniform(ks[13], (DEPTH, N_DIR, S5_GROUPS), jnp.float32, math.log(1e-3), math.log(1e-1))
    inp["s5_b_re"] = nrm(14, (DEPTH, S5_GROUPS, S5_STATE, S5_CH), (2.0 * S5_CH) ** -0.5)
    inp["s5_b_im"] = nrm(15, (DEPTH, S5_GROUPS, S5_STATE, S5_CH), (2.0 * S5_CH) ** -0.5)
    inp["s5_c_re"] = nrm(16, (DEPTH, S5_GROUPS, S5_CH, S5_STATE), (2.0 * S5_STATE) ** -0.5)
    inp["s5_c_im"] = nrm(17, (DEPTH, S5_GROUPS, S5_CH, S5_STATE), (2.0 * S5_STATE) ** -0.5)
    inp["s5_d"] = nrm(18, (DEPTH, S5_WIDTH), 1.0)
    inp["w_glu"] = nrm(19, (DEPTH, S5_WIDTH, S5_WIDTH), S5_WIDTH ** -0.5)
    inp["b_glu"] = nrm(20, (DEPTH, S5_WIDTH), 0.02)
    inp["w_out"] = nrm(21, (DEPTH, MIX_WIDTH, D_MODEL), BETA * MIX_WIDTH ** -0.5)
    inp["ln1_g"] = 1.0 + nrm(22, (DEPTH, D_MODEL), 0.02)
    inp["ln1_b"] = nrm(23, (DEPTH, D_MODEL), 0.02)
    inp["w_ff1"] = nrm(24, (DEPTH, D_MODEL, D_FF), D_MODEL ** -0.5)
    inp["b_ff1"] = nrm(25, (DEPTH, D_FF), 0.02)
    inp["w_ff2"] = nrm(26, (DEPTH, D_FF, D_MODEL), BETA * D_FF ** -0.5)
    inp["b_ff2"] = nrm(27, (DEPTH, D_MODEL), 0.02)
    inp["ln2_g"] = 1.0 + nrm(28, (DEPTH, D_MODEL), 0.02)
    inp["ln2_b"] = nrm(29, (DEPTH, D_MODEL), 0.02)
    return inp


def reference(x_prompt, x_sample, state_ret, state_s5_re, state_s5_im, c, c_ctx,
              w_ada, b_ada, w_in, ret_decay, s5_a_re, s5_a_im, s5_log_dt,
              s5_b_re, s5_b_im, s5_c_re, s5_c_im, s5_d, w_glu, b_glu, w_out,
              ln1_g, ln1_b, w_ff1, b_ff1, w_ff2, b_ff2, ln2_g, ln2_b):
    n_ctx_chunks = x_prompt.shape[1] // CTX_CHUNK
    rows = x_sample.shape[1] // GRID_W
    bp = x_prompt.shape[0]
    zeros_ret = jnp.zeros((bp, N_DIR, RET_HEADS, RET_DK, RET_DV), jnp.float32)
    zeros_s5 = jnp.zeros((bp, N_DIR, S5_GROUPS, S5_STATE), jnp.float32)
    y_p, y_s = x_prompt, x_sample
    rets, s5rs, s5is = [], [], []
    for layer in range(DEPTH):
        params = (w_ada[layer], b_ada[layer], w_in[layer], ret_decay[layer], s5_a_re[layer], s5_a_im[layer],
                  s5_log_dt[layer], s5_b_re[layer], s5_b_im[layer], s5_c_re[layer], s5_c_im[layer], s5_d[layer],
                  w_glu[layer], b_glu[layer], w_out[layer], ln1_g[layer], ln1_b[layer], w_ff1[layer], b_ff1[layer],
                  w_ff2[layer], b_ff2[layer], ln2_g[layer], ln2_b[layer])
        y_p, r_st, s5_re, s5_im = _trunk_layer(y_p, c_ctx[None, :], zeros_ret, zeros_s5, zeros_s5,
                                               CTX_CHUNK, n_ctx_chunks, *params)
        rets.append(r_st)
        s5rs.append(s5_re)
        s5is.append(s5_im)
        y_s, _, _, _ = _trunk_layer(y_s, c, state_ret[:, layer], state_s5_re[:, layer], state_s5_im[:, layer],
                                    GRID_W, rows, *params)
    new_state_ret = jnp.stack(rets, axis=1)
    new_state_s5_re = jnp.stack(s5rs, axis=1)
    new_state_s5_im = jnp.stack(s5is, axis=1)
    return (y_p, y_s, new_state_ret, new_state_s5_re, new_state_s5_im)
```

```python
import numpy as np
from contextlib import ExitStack
import concourse.bass as bass
import concourse.mybir as mybir
from concourse.bass_utils import run_bass_kernel_spmd

F32 = mybir.dt.float32
BF16 = mybir.dt.bfloat16
I32 = mybir.dt.int32
ALU = mybir.AluOpType
AF = mybir.ActivationFunctionType
PI = float(np.pi)
ALPHA = 2.0 ** 0.25
EPS = 1e-5
ENGS = ["pe", "act", "dve", "pool", "sp"]


class Op:
    __slots__ = ("eng", "fn", "deps", "ticket", "signal", "is_dma", "dsem", "dtarget", "hard", "raw", "noinst")


class DSem:
    def __init__(self, sem, batch=False):
        self.sem = sem
        self.count = 0
        self.batch = batch


class Sched:
    def __init__(self, nc, stack):
        self.nc = nc
        self.stack = stack
        self.ops = {e: [] for e in ENGS}
        self.lastw = {}
        self.readers = {}
        self.esem = {e: stack.enter_context(nc.semaphore("s_" + e)) for e in ENGS}
        self.dsems = []
        self.alias = {}

    def expand(self, keys):
        out = []
        for k in keys:
            out.extend(self.alias.get(k, (k,)))
        return out

    def dsem(self, name, batch=False):
        d = DSem(self.stack.enter_context(self.nc.semaphore("d_" + name)), batch)
        self.dsems.append(d)
        return d

    def op(self, eng, fn, r=(), w=(), dsem=None, hard=False):
        o = Op()
        o.hard = hard
        o.noinst = False
        o.eng, o.fn, o.is_dma, o.dsem = eng, fn, dsem is not None, dsem
        o.signal, o.ticket, o.dtarget = False, None, None
        if dsem is not None:
            dsem.count += 16
            o.dtarget = dsem.count
        deps = set()
        raw = set()
        r = self.expand(r)
        w = self.expand(w)
        for b in r:
            lw = self.lastw.get(b)
            if lw is not None:
                deps.add(lw)
                raw.add(lw)
            self.readers.setdefault(b, []).append(o)
        for b in w:
            lw = self.lastw.get(b)
            if lw is not None:
                deps.add(lw)
            for rd in self.readers.get(b, ()):
                deps.add(rd)
            self.lastw[b] = o
            self.readers[b] = []
        deps.discard(o)
        raw.discard(o)
        o.deps = deps
        o.raw = raw
        self.ops[eng].append(o)
        return o

    def finalize(self):
        for e in ENGS:
            for o in self.ops[e]:
                for d in o.deps:
                    if not d.is_dma and (d.eng != o.eng or o.is_dma or o.hard or (o.eng != "pe")):
                        d.signal = True
        for e in ENGS:
            n = 0
            for o in self.ops[e]:
                if o.signal and not o.is_dma:
                    n += 1
                    o.ticket = n

    def emit(self, eng, eh, final_waits=()):
        waited = {}
        for o in self.ops[eng]:
            need = {}
            for d in o.deps:
                if d.is_dma:
                    if o.is_dma and d.dsem is o.dsem and d.dsem.batch:
                        continue
                    sem, val = d.dsem.sem, (d.dsem.count if d.dsem.batch else d.dtarget)
                elif d.eng != eng or o.is_dma or o.hard or (eng != "pe"):
                    sem, val = self.esem[d.eng], d.ticket
                else:
                    continue
                k = id(sem)
                if need.get(k, (None, 0))[1] < val:
                    need[k] = (sem, val)
            if o.is_dma and not o.dsem.batch and o.dtarget > 16:
                k = id(o.dsem.sem)
                if need.get(k, (None, 0))[1] < o.dtarget - 16:
                    need[k] = (o.dsem.sem, o.dtarget - 16)
            for k, (sem, val) in need.items():
                if waited.get(k, 0) < val:
                    eh.wait_ge(sem, val)
                    waited[k] = val
            ins = o.fn(eh)
            if ins is None:
                continue
            if o.is_dma:
                ins.then_inc(o.dsem.sem, 16)
            elif o.signal:
                ins.then_inc(self.esem[eng], 1)
        for d in final_waits:
            if d.count > 0:
                eh.wait_ge(d.sem, d.count)


def rev_ap(ap, dim):
    pat = [list(x) for x in ap.ap]
    st, n = pat[dim]
    off = ap.offset + st * (n - 1)
    pat[dim] = [-st, n]
    return bass.AP(ap.tensor, off, pat)


SKIP_BARRIERS = {2, 3, 4, 6, 7, 8, 9, 10, 12, 14, 15, 16, 17, 18, 20, 23}


def build(dbg=()):
    nc = bass.Bass("TRN2", target_bir_lowering=False)
    stack = ExitStack()
    S = Sched(nc, stack)
    dbg_out = {}

    def din(name, shape):
        return nc.dram_tensor(name, list(shape), F32, kind="ExternalInput").ap()

    def dout(name, shape):
        return nc.dram_tensor(name, list(shape), F32, kind="ExternalOutput").ap()

    xs = din("xs", [2048, 1024])
    xp = din("xp", [512, 1024])
    cond = din("cond", [2, 1024])
    sret = din("sret", [2, 4, 128, 128])
    ss5r = din("ss5r", [2, 32, 64])
    ss5i = din("ss5i", [2, 32, 64])
    w_ada = din("w_ada", [1024, 6144])
    b_ada = din("b_ada", [1, 6144])
    w_in = din("w_in", [1024, 2560])
    ret_decay = din("ret_decay", [1, 8])
    a_re = din("a_re", [2, 32, 64])
    a_im = din("a_im", [2, 32, 64])
    log_dt = din("log_dt", [2, 32])
    b_re = din("b_re", [32, 64, 16])
    b_im = din("b_im", [32, 64, 16])
    c_re = din("c_re", [32, 16, 64])
    c_im = din("c_im", [32, 16, 64])
    s5_d = din("s5_d", [1, 512])
    w_glu = din("w_glu", [512, 512])
    b_glu = din("b_glu", [512])
    w_out = din("w_out", [1024, 1024])
    ln1_g = din("ln1_g", [1, 1024])
    ln1_b = din("ln1_b", [1, 1024])
    w_ff1 = din("w_ff1", [1024, 4096])
    b_ff1 = din("b_ff1", [4096])
    w_ff2 = din("w_ff2", [4096, 1024])
    b_ff2 = din("b_ff2", [1, 1024])
    ln2_g = din("ln2_g", [1, 1024])
    ln2_b = din("ln2_b", [1, 1024])

    yp = dout("yp", [512, 1024])
    ys = dout("ys", [1024, 1024])
    nret = dout("nret", [2, 2, 4, 128, 128])
    ns5r = dout("ns5r", [2, 2, 32, 64])
    ns5i = dout("ns5i", [2, 2, 32, 64])

    def sb(name, shape, dt=F32):
        return stack.enter_context(nc.sbuf_tensor(name, list(shape), dt))

    def ps(name, shape, dt=F32):
        return stack.enter_context(nc.psum_tensor(name, list(shape), dt))

    psT = [ps("psT%d" % i, [128, 1024], BF16) for i in range(2)]
    psD = [ps("psD%d" % i, [128, 1024], F32) for i in range(3)]

    for i_ in range(3):
        for nm_ in ("psy", "pso", "pss", "psfull"):
            S.alias["%s%d" % (nm_, i_)] = ["psD%d_0" % i_, "psD%d_1" % i_]
    for nm_ in ("st6", "mv", "rstd", "nmr"):
        S.alias[nm_] = [nm_ + "_0", nm_ + "_1"]
    S.alias["Sbf"] = ["Sbf0", "Sbf1"]
    for nm_, i_ in (("psmod", 0), ("psg0", 1), ("psg1", 2), ("pss5a", 0), ("pss5c0", 1), ("pss5c1", 2)):
        S.alias[nm_] = ["psD%d_0" % i_, "psD%d_1" % i_]
    def tt(eng, out, a, b, op, r, w):
        return S.op(eng, lambda e: e.tensor_tensor(out=out, in0=a, in1=b, op=op), r, w)

    def ts(eng, out, a, s1, s2, op0, op1, r, w):
        hard = not isinstance(s1, (int, float)) or not (s2 is None or isinstance(s2, (int, float)))
        if s2 is None:
            return S.op(eng, lambda e: e.tensor_scalar(out=out, in0=a, scalar1=s1, scalar2=None, op0=op0), r, w, hard=hard)
        return S.op(eng, lambda e: e.tensor_scalar(out=out, in0=a, scalar1=s1, scalar2=s2, op0=op0, op1=op1), r, w, hard=hard)

    def stt(eng, out, a, sc, b, op0, op1, r, w):
        return S.op(eng, lambda e: e.scalar_tensor_tensor(out=out, in0=a, scalar=sc, in1=b, op0=op0, op1=op1), r, w,
                    hard=not isinstance(sc, (int, float)))

    def cp(eng, out, a, r, w):
        if eng == "act":
            return S.op(eng, lambda e: e.activation(out=out, in_=a, func=AF.Identity), r, w)
        return S.op(eng, lambda e: e.tensor_copy(out=out, in_=a), r, w)

    def act(out, a, func, r, w, bias=None, scale=None):
        kw = {}
        if bias is not None:
            kw["bias"] = bias
        if scale is not None:
            kw["scale"] = scale
        hard = any(not isinstance(v_, (int, float)) for v_ in kw.values())
        return S.op("act", lambda e: e.activation(out=out, in_=a, func=func, **kw), r, w, hard=hard)

    def mm(out, lhsT, rhs, start, stop, r, w):
        return S.op("pe", lambda e: e.matmul(out, lhsT=lhsT, rhs=rhs, start=start, stop=stop), r, w)

    def tr(out, a, ident, r, w):
        return S.op("pe", lambda e: e.transpose(out=out, in_=a, identity=ident), r, w)

    def dma(q, out, a, dsem, r, w, noncontig=False):
        if noncontig:
            return S.op(q, lambda e: e.dma_start(out=out, in_=a, allow_slow_non_contiguous=True), r, w, dsem=dsem)
        return S.op(q, lambda e: e.dma_start(out=out, in_=a), r, w, dsem=dsem)

    def memset(eng, ap, val, w):
        return S.op(eng, lambda e: e.memset(ap, val), (), w)

    d_out = S.dsem("out")
    d_ld0 = S.dsem("ld0", batch=True)
    G = {nm_: S.dsem("g_" + nm_, batch=True) for nm_ in ("cond", "bada", "rd", "ain", "ldt", "bt", "cin", "dbc", "ld4", "ld5", "st0", "st1")}
    d_ld1 = S.dsem("ld1", batch=True)
    d_ld2 = S.dsem("ld2", batch=True)
    d_ld3 = S.dsem("ld3", batch=True)
    d_oy = [S.dsem("oy0"), S.dsem("oy1")]

    def finish():
        dbg_dram = {}
        for name, ent in dbg_out.items():
            ap, shape, keys = ent[0], ent[1], ent[2]
            dbg_dram[name] = dout("dbg_" + name, shape)
            q_ = "pool" if (len(ent) > 3 and ent[3] == BF16) else "sp"
            dma(q_, dbg_dram[name][:, :], ap, d_out, list(keys) + ["BAR"], [])

        S.finalize()
        with nc.Block() as block:
            @block.sync
            def _(e):
                S.emit("sp", e, final_waits=S.dsems)

            @block.scalar
            def _(e):
                S.emit("act", e)

            @block.vector
            def _(e):
                S.emit("dve", e)

            @block.gpsimd
            def _(e):
                S.emit("pool", e)

            @block.tensor
            def _(e):
                S.emit("pe", e)
        stack.close()
        return nc, list(dbg_dram.keys())

    AKB = 194
    arena = sb("arena", [128, AKB * 256], F32)

    def carve(off_kb, shape, dt=F32, p0=0):
        n = int(np.prod(shape[1:]))
        words = n if dt == F32 else (n + 1) // 2
        w0 = int(round(off_kb * 256))
        assert w0 + words <= AKB * 256, (off_kb, shape)
        ap = arena[p0:p0 + shape[0], w0:w0 + words]
        if dt != F32:
            ap = ap.bitcast(dt)
        if len(shape) > 2:
            names = " ".join("d%d" % i for i in range(1, len(shape)))
            ap = ap.rearrange("p (%s) -> p %s" % (names, names), **{"d%d" % i: shape[i] for i in range(1, len(shape) - 1)})
        return ap

    def pat(ap, dims, off=0):
        return bass.AP(ap.tensor, ap.offset + off, [list(ap.ap[0])] + [list(d) for d in dims])

    def rowbc(ap_row, P):
        pt = [list(x) for x in ap_row.ap]
        if len(pt) == 2:
            pt = pt[1:]
        return bass.AP(ap_row.tensor, ap_row.offset, [[0, P]] + pt)

    ident_f = sb("ident_f", [128, 128], F32)
    ident_b = sb("ident_b", [128, 128], BF16)
    ones_f = sb("ones_f", [128, 128], F32)
    iota_i = sb("iota_i", [128, 128], I32)
    ii = sb("ii", [128, 128], F32)
    jj = sb("jj", [128, 1], F32)
    dmat = sb("dmat", [128, 128], F32)
    bar_t = sb("bar_t", [128, 8], F32)
    epsc = sb("epsc", [128, 1], F32)
    itmp = sb("itmp", [128, 256], I32)
    ftmp_holder = [None]

    S.op("pool", lambda e: e.iota(iota_i[:], pattern=[[1, 128]], base=0, channel_multiplier=0), (), ["iota_i"])
    cp("dve", ii[:], iota_i[:], ["iota_i"], ["ii"])
    S.op("pool", lambda e: e.iota(iota_i[:, 0:1], pattern=[[0, 1]], base=0, channel_multiplier=1), ["iota_i"], ["iota_i"])
    cp("dve", jj[:], iota_i[:, 0:1], ["iota_i"], ["jj"])
    ts("dve", dmat[:], ii[:], jj[:, 0:1], None, ALU.subtract, None, ["ii", "jj"], ["dmat"])
    ts("dve", ident_f[:], dmat[:], 0.0, None, ALU.is_equal, None, ["dmat"], ["ident_f"])
    cp("dve", ident_b[:], ident_f[:], ["ident_f"], ["ident_b"])
    memset("dve", ones_f[:], 1.0, ["ones_f"])
    memset("dve", epsc[:], EPS, ["epsc"])

    if "stop_const" in dbg:
        dbg_out["ii"] = (ii[:], [128, 128], ["ii"], F32)
        dbg_out["jj"] = (jj[:], [128, 1], ["jj"], F32)
        dbg_out["dmat"] = (dmat[:], [128, 128], ["dmat"], F32)
        dbg_out["ident"] = (ident_f[:], [128, 128], ["ident_f"], F32)
        return finish()
    bar_n = [0]

    def barrier():
        bar_n[0] += 1
        if bar_n[0] in SKIP_BARRIERS:
            return
        a_ops = []
        for e_ in ("pe", "act", "dve", "pool"):
            for o_ in reversed(S.ops[e_]):
                if not o_.noinst:
                    a_ops.append(o_)
                    break
        last_dma = {}
        for q in ENGS:
            for o in S.ops[q]:
                if o.is_dma:
                    last_dma[id(o.dsem)] = o
        b1 = S.op("dve", lambda e: e.memset(bar_t[:, 0:1], 0.0), (), ["BAR1"])
        b1.deps |= set(a_ops) | set(last_dma.values())
        b1.deps.discard(b1)
        b1.hard = True
        b = S.op("act", lambda e: e.activation(out=bar_t[:, 1:2], in_=ones_f[:, 0:1], func=AF.Identity), ["BAR1"], ["BAR"])
        b.deps |= set(a_ops)
        b.deps.discard(b)
        b.hard = True
        S.op("pool", lambda e: e.memset(bar_t[:, 2:3], 0.0), ["BAR"], [])
        S.op("dve", lambda e: e.memset(bar_t[:, 3:4], 0.0), ["BAR"], [])
        S.op("pe", lambda e: None, ["BAR"], []).noinst = True
        S.lastw.clear()
        S.readers.clear()
        S.lastw["BAR"] = b

    BARK = ["BAR"]

    maskT = carve(0, [128, 4, 128])
    colf = carve(2, [128, 4, 128])
    colb = carve(4, [128, 4, 128])
    kdf_tab = carve(6, [128, 4, 128])
    kdb_tab = carve(8, [128, 4, 128])
    dec_tab = carve(10, [128, 2, 4, 128])
    Mtoe = carve(14, [128, 32, 128], BF16)
    WsR = carve(22, [128, 32, 128], BF16)
    WsI = carve(30, [128, 32, 128], BF16)
    CoR = carve(38, [128, 32, 128], BF16)
    CoI = carve(46, [128, 32, 128], BF16)
    g1bc = carve(54, [128, 2, 1024])
    g2bc = carve(62, [128, 2, 1024])

    TWO_PI = 2.0 * PI
    a_in = [carve(50 + 0.5 * i, [32, 128]) for i in range(4)]
    for t_, src in zip(a_in, (a_re, a_im, ss5r, ss5i)):
        dma("sp", t_.rearrange("g (d p) -> g d p", d=2), src.rearrange("d g p -> g d p"), G["ain"], (), ["a_in"])
    ldt = sb("ldt", [128, 32])
    for d_ in range(2):
        dma("sp", ldt[d_ * 64:(d_ + 1) * 64, :], rowbc(log_dt[d_:d_ + 1, :], 64), G["ldt"], (), ["ldt"])
    Bt = [carve(174, [128, 32, 16]), carve(176, [128, 32, 16])]
    Bbar = [carve(178, [128, 32, 16]), carve(180, [128, 32, 16])]
    Ct = [carve(182, [128, 32, 16]), carve(184, [128, 32, 16])]
    Cin = [carve(186, [128, 4, 128]), carve(188, [128, 4, 128])]
    dbc = carve(190, [128, 32, 16])
    for ri, src in enumerate((b_re, b_im)):
        for d_ in range(2):
            dma("sp", Bt[ri][d_ * 64:(d_ + 1) * 64], src.rearrange("g p c -> p g c"), G["bt"], (), ["Bt"], noncontig=True)
    for ri, src in enumerate((c_re, c_im)):
        for dup in range(2):
            dma("sp", Cin[ri][:, :, dup * 64:(dup + 1) * 64], src.rearrange("(blk gg) c p -> (gg c) blk p", blk=4),
                G["cin"], (), ["Cin"])
    dma("sp", dbc.rearrange("p g c -> p (g c)"), rowbc(s5_d, 128), G["dbc"], (), ["dbc"])

    sm = {}
    for nm in ("ar", "ai", "s0r", "s0i", "dt", "lr", "li", "abr", "abi", "den", "nr", "ni", "cr", "ci", "t1", "t2", "t3",
               "p8r", "p8i", "n7r", "n7i", "p7r", "p7i", "e7"):
        sm[nm] = sb("sm_" + nm, [128, 32])
    for k_, nm in enumerate(("ar", "ai", "s0r", "s0i")):
        mm(psD[0][:, k_ * 32:(k_ + 1) * 32], a_in[k_], ident_f[0:32, 0:32], True, True, ["a_in", "ident_f"], ["pss5a"])
        cp("dve", sm[nm][:], psD[0][:, k_ * 32:(k_ + 1) * 32], ["pss5a"], ["sm_" + nm])
    for ri in range(2):
        for blk in range(4):
            mm(psD[1 + ri][:, blk * 128:(blk + 1) * 128], Cin[ri][:, blk, :], ident_f[:], True, True, ["Cin", "ident_f"], ["pss5c%d" % ri])
        cp("act", Ct[ri].rearrange("p g c -> p (g c)"), psD[1 + ri][:, 0:512], ["pss5c%d" % ri], ["Ct"])

    def K(*names):
        return ["sm_" + n for n in names]

    act(sm["dt"][:], ldt[:], AF.Exp, ["ldt"], K("dt"))
    tt("dve", sm["lr"][:], sm["ar"][:], sm["dt"][:], ALU.mult, K("ar", "dt"), K("lr"))
    tt("dve", sm["li"][:], sm["ai"][:], sm["dt"][:], ALU.mult, K("ai", "dt"), K("li"))

    condT = sb("condT", [128, 8, 2], F32)
    scT = sb("scT", [128, 8, 2], BF16)
    modT = sb("modT", [128, 4, 8, 2], F32)
    bada_f = carve(86, [1, 6144])

    for i in range(2):
        dma("sp", condT[:, :, i], cond[i].rearrange("(kc p) -> p kc", p=128), G["cond"], (), ["condT"], noncontig=True)
    dma("sp", bada_f, b_ada[:, :], G["bada"], (), ["bada_f"])
    act(scT[:], condT[:], AF.Silu, ["condT"], ["scT"])

    wada = [carve(70, [128, 8, 512], BF16), carve(78, [128, 8, 512], BF16)]
    d_wada = [S.dsem("wada0"), S.dsem("wada1")]
    w_ada_v = w_ada.rearrange("(kc p) n -> p kc n", p=128)
    pmod = psT[0][:].bitcast(F32)[:, 0:64]
    gbanks = [(pg_, ["psT1"]) for pg_ in [psT[1][:].bitcast(F32)]]
    gbanks += [(psD[i_][:, h_ * 512:(h_ + 1) * 512], ["psD%d_%d" % (i_, h_)]) for i_ in range(3) for h_ in range(2)]
    gcnt = [0]
    gpend = []
    pg = psT[1][:].bitcast(F32)
    order = [4, 5, 10, 11, 0, 1, 2, 3, 6, 7, 8, 9]
    for n_, pc in enumerate(order):
        v, hh = pc // 2, pc % 2
        bufi = n_ % 2
        wt = wada[bufi]
        key = "wada%d" % bufi
        c0 = v * 1024 + hh * 512
        dma("pool", wt, w_ada_v[:, :, c0:c0 + 512], d_wada[bufi], (), [key])
        if v in (0, 1, 3, 4):
            vi = {0: 0, 1: 1, 3: 2, 4: 3}[v]
            for f4 in range(4):
                fc = hh * 4 + f4
                o_ap = pmod[:, (vi * 8 + fc) * 2:(vi * 8 + fc) * 2 + 2]
                for kc in range(8):
                    mm(o_ap, wt[:, kc, f4 * 128:(f4 + 1) * 128], scT[:, kc, :], kc == 0, False, [key, "scT"], ["psT0"])
                mm(o_ap, bada_f[0:1, c0 + f4 * 128: c0 + (f4 + 1) * 128], ones_f[0:1, 0:2], False, True,
                   ["bada_f", "ones_f"], ["psT0"])
        else:
            gt = g1bc if v == 2 else g2bc
            for i in range(2):
                bank, bkeys = gbanks[gcnt[0] % len(gbanks)]
                gcnt[0] += 1
                for kc in range(8):
                    lhs = pat(scT[:, kc, i:i + 1], [[0, 128]])
                    mm(bank, lhs, wt[:, kc, :], kc == 0, False, [key, "scT"], bkeys)
                mm(bank, ones_f[0:1, 0:128], bada_f[0:1, c0:c0 + 512], False, True, ["bada_f", "ones_f"], bkeys)
                dst_ = gt[:, i, hh * 512:(hh + 1) * 512]
                if gcnt[0] == 1:
                    cp("act", dst_, bank, bkeys, ["gbc"])
                else:
                    gpend.append((dst_, bank, bkeys))

    def finish_g():
        for dst_, bank, bkeys in gpend:
            cp("act", dst_, bank, bkeys, ["gbc"])

    def finish_mod():
        cp("dve", modT[:].rearrange("p v f i -> p (v f i)"), pmod, ["psT0"], ["modT"])
        for vi in (1, 3):
            ts("dve", modT[:, vi], modT[:, vi], 1.0, None, ALU.add, None, ["modT"], ["modT"])

    if "mod" in dbg:
        dbg_out["modT"] = (modT[:].rearrange("p v f i -> p (v f i)"), [128, 64], ["modT"])
        dbg_out["g1bc"] = (g1bc[0:1].rearrange("p a n -> p (a n)"), [1, 2048], ["gbc"])

    if "stop_mod" in dbg:
        return finish()
    rd = sb("rd", [128, 8])
    lg = sb("lg", [128, 8])
    kcol = sb("kcol", [128, 8])
    dcol = sb("dcol", [128, 8])
    tq = [carve(46 + 0.5 * i, [128, 128]) for i in range(6)]
    dpos, dneg, gem, lem, ip1, rmi = tq
    jrev = sb("jrev", [128, 1])
    ftmp = carve(49, [128, 256])
    tmpa = carve(49, [128, 128])
    tmpb = carve(49.5, [128, 128])
    SC = 128.0 ** -0.5
    dma("sp", rd[:], rowbc(ret_decay, 128), G["rd"], (), ["rd"])
    act(lg[:], rd[:], AF.Exp, ["rd"], ["lg"], scale=-1.0)
    act(lg[:], lg[:], AF.Ln, ["lg"], ["lg"], bias=ones_f[:, 0:1])
    ts("dve", lg[:], lg[:], -1.0, None, ALU.mult, None, ["lg"], ["lg"])
    ts("dve", dpos, dmat[:], 0.0, None, ALU.max, None, ["dmat"], ["tq"])
    ts("dve", dneg, dmat[:], -1.0, 0.0, ALU.mult, ALU.max, ["dmat"], ["tq"])
    ts("dve", gem, dmat[:], 0.0, None, ALU.is_ge, None, ["dmat"], ["tq"])
    ts("dve", lem, dmat[:], 0.0, None, ALU.is_le, None, ["dmat"], ["tq"])
    ts("dve", ip1, ii[:], 1.0, None, ALU.add, None, ["ii"], ["tq"])
    ts("dve", rmi, ii[:], -1.0, 128.0, ALU.mult, ALU.add, ["ii"], ["tq"])
    ts("dve", jrev[:], jj[:], -1.0, 127.0, ALU.mult, ALU.add, ["jj"], ["jrev"])
    for h in range(4):
        act(tmpa, dpos, AF.Exp, ["tq", "lg"], ["tmpa"], scale=lg[:, h:h + 1])
        tt("dve", tmpa, tmpa, gem, ALU.mult, ["tmpa", "tq"], ["tmpa"])
        act(tmpb, dneg, AF.Exp, ["tq", "lg"], ["tmpb"], scale=lg[:, 4 + h:5 + h])
        stt("dve", tmpb, tmpb, SC, lem, ALU.mult, ALU.mult, ["tmpb", "tq"], ["tmpb"])
        stt("dve", maskT[:, h, :], tmpa, SC, tmpb, ALU.mult, ALU.add, ["tmpa", "tmpb"], ["rtab"])
        act(colf[:, h, :], ip1, AF.Exp, ["tq", "lg"], ["rtab"], scale=lg[:, h:h + 1])
        act(colb[:, h, :], rmi, AF.Exp, ["tq", "lg"], ["rtab"], scale=lg[:, 4 + h:5 + h])
        act(kcol[:, h:h + 1], jrev[:], AF.Exp, ["jrev", "lg"], ["kcol"], scale=lg[:, h:h + 1])
        act(kcol[:, 4 + h:5 + h], jj[:], AF.Exp, ["jj", "lg"], ["kcol"], scale=lg[:, 4 + h:5 + h])
    act(dcol[:], lg[:], AF.Exp, ["lg"], ["dcol"], scale=128.0)
    ts("dve", kcol[:], kcol[:], SC, None, ALU.mult, None, ["kcol"], ["kcol"])
    cp("dve", kdf_tab, pat(kcol[:, 0:4], [[1, 4], [0, 128]]), ["kcol"], ["rtab"])
    cp("dve", kdb_tab, pat(kcol[:, 4:8], [[1, 4], [0, 128]]), ["kcol"], ["rtab"])
    cp("dve", dec_tab.rearrange("p d h e -> p (d h) e"), pat(dcol[:, 0:8], [[1, 8], [0, 128]]), ["dcol"], ["rtab"])
    if "stop_ret" in dbg:
        dbg_out["lg"] = (lg[:], [128, 8], ["lg"], F32)
        dbg_out["kcol"] = (kcol[:], [128, 8], ["kcol"], F32)
        dbg_out["colf"] = (colf.rearrange("p h i -> p (h i)"), [128, 512], ["rtab"], F32)
        dbg_out["maskT"] = (maskT.rearrange("p h i -> p (h i)"), [128, 512], ["rtab"], F32)
        return finish()
    def cplx_pow(out_r, out_i, er, ei, shape_n, keys_r, keys_w, tmp, ft_=None):
        ftmp_l = ftmp if ft_ is None else ft_
        mag, c_, s_ = tmp
        act(mag, er, AF.Exp, keys_r, keys_w)
        def flat(a):
            return a if len(a.shape) == 2 else a.rearrange("p a b -> p (a b)")
        n_ = int(np.prod(er.shape[1:]))
        for dstt, shift in ((s_, 64.0), (c_, 64.25)):
            d2 = flat(dstt)
            ts("dve", d2, flat(ei), 1.0 / TWO_PI, shift, ALU.mult, ALU.add, keys_r, keys_w)
            cp("dve", itmp[:, 0:n_], d2, keys_w, keys_w + ["itmp"])
            cp("dve", ftmp_l[:, 0:n_], itmp[:, 0:n_], ["itmp"], ["tmpa", "tmpb"])
            tt("dve", d2, d2, ftmp_l[:, 0:n_], ALU.subtract, keys_w + ["tmpa", "tmpb"], keys_w)
            ts("dve", ftmp_l[:, 0:n_], d2, 0.5, None, ALU.is_ge, None, keys_w, ["tmpa", "tmpb"])
            tt("dve", d2, d2, ftmp_l[:, 0:n_], ALU.subtract, keys_w + ["tmpa", "tmpb"], keys_w)
            act(d2, d2, AF.Sin, keys_w, keys_w, scale=TWO_PI)
        tt("dve", out_r, mag, c_, ALU.mult, keys_w, keys_w)
        tt("dve", out_i, mag, s_, ALU.mult, keys_w, keys_w)

    if "stop_s5a" in dbg:
        dbg_out["lr"] = (sm["lr"][:], [128, 32], ["sm_lr"], F32)
        dbg_out["Ct0"] = (Ct[0].rearrange("p g c -> p (g c)"), [128, 512], ["Ct"], F32)
        return finish()
    cplx_pow(sm["abr"][:], sm["abi"][:], sm["lr"][:], sm["li"][:], None, K("lr", "li"), K("abr", "abi", "t1", "t2", "t3"),
             (sm["t1"][:], sm["t2"][:], sm["t3"][:]))
    if "stop_s5b" in dbg:
        for nm_ in ("ar", "ai", "dt", "lr", "li", "abr", "abi", "t1", "t2", "t3"):
            dbg_out[nm_] = (sm[nm_][:], [128, 32], ["sm_" + nm_], F32)
        return finish()
    tt("dve", sm["den"][:], sm["ar"][:], sm["ar"][:], ALU.mult, K("ar"), K("den"))
    tt("dve", sm["t1"][:], sm["ai"][:], sm["ai"][:], ALU.mult, K("ai"), K("t1"))
    tt("dve", sm["den"][:], sm["den"][:], sm["t1"][:], ALU.add, K("den", "t1"), K("den"))
    S.op("dve", lambda e: e.reciprocal(out=sm["den"][:], in_=sm["den"][:]), K("den"), K("den"))
    ts("dve", sm["t2"][:], sm["abr"][:], -1.0, None, ALU.add, None, K("abr"), K("t2"))
    tt("dve", sm["nr"][:], sm["t2"][:], sm["ar"][:], ALU.mult, K("t2", "ar"), K("nr"))
    tt("dve", sm["t1"][:], sm["abi"][:], sm["ai"][:], ALU.mult, K("abi", "ai"), K("t1"))
    tt("dve", sm["nr"][:], sm["nr"][:], sm["t1"][:], ALU.add, K("nr", "t1"), K("nr"))
    tt("dve", sm["ni"][:], sm["abi"][:], sm["ar"][:], ALU.mult, K("abi", "ar"), K("ni"))
    tt("dve", sm["t1"][:], sm["t2"][:], sm["ai"][:], ALU.mult, K("t2", "ai"), K("t1"))
    tt("dve", sm["ni"][:], sm["ni"][:], sm["t1"][:], ALU.subtract, K("ni", "t1"), K("ni"))
    tt("dve", sm["cr"][:], sm["nr"][:], sm["den"][:], ALU.mult, K("nr", "den"), K("cr"))
    tt("dve", sm["ci"][:], sm["ni"][:], sm["den"][:], ALU.mult, K("ni", "den"), K("ci"))
    crb = pat(sm["cr"][:], [[1, 32], [0, 16]])
    cib = pat(sm["ci"][:], [[1, 32], [0, 16]])
    tb0 = carve(192, [128, 32, 16])
    tt("dve", Bbar[0], Bt[0], crb, ALU.mult, ["Bt"] + K("cr"), ["Bbar"])
    tt("dve", tb0, Bt[1], cib, ALU.mult, ["Bt"] + K("ci"), ["tb0"])
    tt("dve", Bbar[0], Bbar[0], tb0, ALU.subtract, ["Bbar", "tb0"], ["Bbar"])
    tt("dve", Bbar[1], Bt[1], crb, ALU.mult, ["Bt"] + K("cr"), ["Bbar"])
    tt("dve", tb0, Bt[0], cib, ALU.mult, ["Bt", "Bbar"] + K("ci"), ["tb0"])
    tt("dve", Bbar[1], Bbar[1], tb0, ALU.add, ["Bbar", "tb0"], ["Bbar"])

    EL = sb("EL", [128, 8])
    ER = sb("ER", [128, 8])
    ts("dve", EL[0:64, :], ii[0:64, 0:8], -1.0, None, ALU.mult, None, ["ii"], ["EL"])
    cp("dve", EL[64:128, :], ii[64:128, 0:8], ["ii"], ["EL"])
    ts("dve", ER[:], EL[:], -1.0, None, ALU.mult, None, ["EL"], ["ER"])
    E8 = sb("E8", [128, 8])
    ts("dve", E8[:], ii[:, 0:8], 1.0, 8.0, ALU.add, ALU.mult, ["ii"], ["E8"])
    PW = {}
    ptmp = [carve(18, [128, 8, 32]), carve(19, [128, 8, 32]), carve(20, [128, 8, 32]), carve(21, [128, 8, 32])]
    for nm, E_, off in (("L", EL, 14), ("R", ER, 16)):
        pr = carve(off, [128, 8, 32])
        pi_ = carve(off + 1, [128, 8, 32])
        Eb = pat(E_[:], [[1, 8], [0, 32]])
        tt("dve", ptmp[0], Eb, pat(sm["lr"][:], [[0, 8], [1, 32]]), ALU.mult, [nm == "L" and "EL" or "ER"] + K("lr"), ["ptmp"])
        tt("dve", ptmp[1], Eb, pat(sm["li"][:], [[0, 8], [1, 32]]), ALU.mult, [nm == "L" and "EL" or "ER"] + K("li"), ["ptmp"])
        cplx_pow(pr, pi_, ptmp[0], ptmp[1], None, ["ptmp"], ["ptmp", "P" + nm], (ptmp[2], ptmp[3], ptmp[0]))
        PW[nm] = (pr, pi_)

    def single_pow(nr_, ni_, e_lo, e_hi):
        for lo, hi, ev in ((0, 64, e_lo), (64, 128, e_hi)):
            ts("dve", sm["t1"][lo:hi], sm["lr"][lo:hi], float(ev), None, ALU.mult, None, K("lr"), K("t1"))
            ts("dve", sm["t2"][lo:hi], sm["li"][lo:hi], float(ev), None, ALU.mult, None, K("li"), K("t2"))
        cplx_pow(sm[nr_][:], sm[ni_][:], sm["t1"][:], sm["t2"][:], None, K("t1", "t2"), K(nr_, ni_, "t3", "den", "nr"),
                 (sm["t3"][:], sm["den"][:], sm["nr"][:]))

    single_pow("p8r", "p8i", 8, 8)
    single_pow("n7r", "n7i", -7, 0)
    single_pow("p7r", "p7i", 7, 0)

    Lr = carve(110, [128, 32, 8, 16])
    Li = carve(126, [128, 32, 8, 16])
    Rr = carve(142, [128, 32, 8, 16])
    Rn = carve(158, [128, 32, 8, 16])
    tbig = carve(70, [128, 32, 8, 16])

    def bc_x(x):
        return pat(x, [[16, 32], [0, 8], [1, 16]])

    def bc_p(p_):
        return pat(p_, [[1, 32], [32, 8], [0, 16]])

    WKEY = ["wada0", "wada1"]
    for (Xr, Xi, (Pr, Pi_), Or, Oi, neg, okey) in ((Bbar[0], Bbar[1], PW["L"], Lr, Li, False, "Lset"),
                                                  (Ct[0], Ct[1], PW["R"], Rr, Rn, True, "Rset")):
        xk = ["Bbar"] if okey == "Lset" else ["Ct"]
        pk = ["PL"] if okey == "Lset" else ["PR"]
        tt("dve", Or, bc_x(Xr), bc_p(Pr), ALU.mult, xk + pk, [okey + "r"])
        tt("dve", tbig, bc_x(Xi), bc_p(Pi_), ALU.mult, xk + pk, WKEY)
        tt("dve", Or, Or, tbig, ALU.subtract, [okey + "r"] + WKEY, [okey + "r"])
        tt("dve", Oi, bc_x(Xr), bc_p(Pi_), ALU.mult, xk + pk, [okey + "i"])
        tt("dve", tbig, bc_x(Xi), bc_p(Pr), ALU.mult, xk + pk + WKEY, WKEY)
        if neg:
            stt("dve", Oi, Oi, -1.0, tbig, ALU.mult, ALU.subtract, [okey + "i"] + WKEY, [okey + "i"])
        else:
            tt("dve", Oi, Oi, tbig, ALU.add, [okey + "i"] + WKEY, [okey + "i"])

    p8rb = pat(sm["p8r"][:], [[1, 32], [0, 128]])
    p8ib = pat(sm["p8i"][:], [[1, 32], [0, 128]])
    Rr3 = Rr.rearrange("p g t c -> p g (t c)")
    Rn3 = Rn.rearrange("p g t c -> p g (t c)")
    tb3 = tbig.rearrange("p g t c -> p g (t c)")
    tb4 = carve(86, [128, 32, 128])
    tt("dve", tb3, Rr3, p8rb, ALU.mult, ["Rsetr"] + K("p8r") + WKEY, WKEY)
    tt("dve", tb4, Rn3, p8ib, ALU.mult, ["Rseti", "bada_f"] + K("p8i"), ["bada_f"])
    tt("dve", CoR, tb3, tb4, ALU.add, WKEY + ["bada_f"], ["CoR"])
    tt("dve", tb3, Rn3, p8rb, ALU.mult, ["Rseti"] + K("p8r") + WKEY, WKEY)
    tt("dve", tb4, Rr3, p8ib, ALU.mult, ["Rsetr", "bada_f"] + K("p8i"), ["bada_f"])
    tt("dve", CoI, tb3, tb4, ALU.subtract, WKEY + ["bada_f"], ["CoI", "tq", "tmpa", "tmpb", "a_in"])

    finish_mod()
    finish_g()
    Lb = [carve(94, [128, 32, 128], BF16), carve(102, [128, 32, 128], BF16)]
    cp("dve", Lb[0], Lr.rearrange("p g m c -> p g (m c)"), ["Lsetr"], ["Lb", "bada_f"])
    cp("act", Lb[1], Li.rearrange("p g m c -> p g (m c)"), ["Lseti"], ["Lb", "bada_f"])
    Lr3 = Lr.rearrange("p g m c -> p g (m c)")
    Li3 = Li.rearrange("p g m c -> p g (m c)")
    for ri, Wd in enumerate((WsR, WsI)):
        for g8 in range(4):
            pb = psT[(ri * 4 + g8) % 2]
            pkk = "psT%d" % ((ri * 4 + g8) % 2)
            for k_ in range(8):
                tr(pb[:, k_ * 128:(k_ + 1) * 128], Lb[ri][:, g8 * 8 + k_, :], ident_b[:], ["Lb", "ident_b"], [pkk])
            cp("act" if g8 % 2 else "dve", Wd[:, g8 * 8:(g8 + 1) * 8, :].rearrange("p g n -> p (g n)"), pb[:], [pkk], ["Ws"])
    pm = sb("pm", [128, 1])
    ft = sb("ft", [128, 8])
    mF = sb("mF", [128, 8])
    mB = sb("mB", [128, 8])
    S.op("pool", lambda e: e.iota(iota_i[:, 0:1], pattern=[[0, 1]], base=0, channel_multiplier=1), ["iota_i"], ["iota_i"])
    S.op("dve", lambda e: e.tensor_single_scalar(out=iota_i[:, 1:2], in_=iota_i[:, 0:1], scalar=4, op=ALU.arith_shift_right),
         ["iota_i"], ["iota_i"])
    cp("dve", pm[:], iota_i[:, 1:2], ["iota_i"], ["pm"])
    ts("dve", ft[:], ii[:, 0:8], pm[:, 0:1], None, ALU.subtract, None, ["ii", "pm"], ["ft"])
    ts("dve", mF[:], ft[:], 0.0, None, ALU.is_ge, None, ["ft"], ["mF"])
    ts("dve", mB[:], ft[:], 0.0, None, ALU.is_le, None, ["ft"], ["mB"])
    mFb = pat(mF[:], [[1, 8], [0, 16]])
    mBb = pat(mB[:], [[1, 8], [0, 16]])
    tmA = carve(192, [128, 8, 16])
    tmB = carve(192.5, [128, 8, 16])
    idv = ident_f[:].rearrange("p (t c) -> p t c", t=8)
    mlo = sb("mlo", [128, 2])
    ts("dve", mlo[:, 0:1], jj[:], 64.0, None, ALU.is_lt, None, ["jj"], ["mlo"])
    ts("dve", mlo[:, 1:2], jj[:], 64.0, None, ALU.is_ge, None, ["jj"], ["mlo"])
    mlo_b = pat(mlo[:, 0:1], [[0, 128]])
    mhi_b = pat(mlo[:, 1:2], [[0, 128]])
    mlo4 = pat(mlo[:, 0:1], [[0, 4], [0, 128]])
    mhi4 = pat(mlo[:, 1:2], [[0, 4], [0, 128]])
    mF4 = pat(mF[:, 0:1], [[0, 4], [1, 8], [0, 16]])
    mB4 = pat(mB[:, 0:1], [[0, 4], [1, 8], [0, 16]])
    id4 = pat(ident_f[:, 0:1], [[0, 4], [16, 8], [1, 16]])
    for g4 in range(8):
        par = g4 % 2
        gs = slice(g4 * 4, g4 * 4 + 4)
        rm = [carve(70 + 4 * par + k_, [128, 4, 128], BF16) for k_ in range(4)]
        rk = "rmask%d" % par
        tt("dve", rm[0], Rr3[:, gs, :], mlo4, ALU.mult, ["Rsetr", "mlo"], [rk] + (WKEY if g4 < 2 else []))
        tt("pool", rm[1], Rn3[:, gs, :], mlo4, ALU.mult, ["Rseti", "mlo"], [rk])
        tt("dve", rm[2], Rr3[:, gs, :], mhi4, ALU.mult, ["Rsetr", "mlo"], [rk])
        tt("pool", rm[3], Rn3[:, gs, :], mhi4, ALU.mult, ["Rseti", "mlo"], [rk])
        pb = psD[1 + par][:]
        pk = "psfull%d" % (1 + par)
        pbv = pb.rearrange("p (g f n) -> p g f n", g=4, f=2)
        for k_ in range(4):
            g_ = g4 * 4 + k_
            mm(pbv[:, k_, 0, :], Lb[0][:, g_, :], rm[0][:, k_, :], True, False, ["Lb", rk], [pk])
            mm(pbv[:, k_, 0, :], Lb[1][:, g_, :], rm[1][:, k_, :], False, True, ["Lb", rk], [pk])
            mm(pbv[:, k_, 1, :], Lb[0][:, g_, :], rm[2][:, k_, :], True, False, ["Lb", rk], [pk])
            mm(pbv[:, k_, 1, :], Lb[1][:, g_, :], rm[3][:, k_, :], False, True, ["Lb", rk], [pk])
        tA = carve(86 + 4 * par, [128, 4, 8, 16])
        tB = carve(88 + 4 * par, [128, 4, 8, 16])
        ka, kb = "tmA%d" % par, "tmB%d" % par
        tt("dve", tA, pbv[:, :, 0, :].rearrange("p g (t c) -> p g t c", t=8), mF4, ALU.mult, [pk, "mF"], [ka, "bada_f"])
        tt("dve", tB, pbv[:, :, 1, :].rearrange("p g (t c) -> p g t c", t=8), mB4, ALU.mult, [pk, "mB"], [kb, "bada_f"])
        tt("dve", tA, tA, tB, ALU.add, [ka, kb], [ka])
        tt("dve", tB, id4, pat(dbc[:, g4 * 4, 0:1], [[16, 4], [0, 8], [1, 16]]), ALU.mult, ["ident_f", "dbc", kb], [kb])
        tt("pool", Mtoe[:, gs, :].rearrange("p g (t c) -> p g t c", t=8), tA, tB, ALU.add, [ka, kb], ["Mtoe", "PL", "PR", "ptmp"])

    AA = sb("AA", [128, 2, 32])
    AIs = sb("AIs", [128, 2, 32])
    cp("dve", AA[:], pat(sm["p8r"][:], [[0, 2], [1, 32]]), K("p8r"), ["AA"])
    ts("dve", AIs[:, 0, :], sm["p8i"][:], -1.0, None, ALU.mult, None, K("p8i"), ["AIs"])
    cp("dve", AIs[:, 1, :], sm["p8i"][:], K("p8i"), ["AIs"])
    Z0 = sb("Z0", [128, 2, 32])
    tt("dve", Z0[:, 0, :], sm["s0r"][:], sm["n7r"][:], ALU.mult, K("s0r", "n7r"), ["Z0"])
    tt("dve", sm["t1"][:], sm["s0i"][:], sm["n7i"][:], ALU.mult, K("s0i", "n7i"), K("t1"))
    tt("dve", Z0[:, 0, :], Z0[:, 0, :], sm["t1"][:], ALU.subtract, ["Z0"] + K("t1"), ["Z0"])
    tt("dve", Z0[:, 1, :], sm["s0r"][:], sm["n7i"][:], ALU.mult, K("s0r", "n7i"), ["Z0"])
    tt("dve", sm["t1"][:], sm["s0i"][:], sm["n7r"][:], ALU.mult, K("s0i", "n7r"), K("t1"))
    tt("dve", Z0[:, 1, :], Z0[:, 1, :], sm["t1"][:], ALU.add, ["Z0"] + K("t1"), ["Z0"])

    if "s5mat" in dbg:
        dbg_out["Mtoe"] = (Mtoe[:, 0:4, :].rearrange("p g n -> p (g n)"), [128, 512], ["Mtoe"], BF16)
        dbg_out["WsR"] = (WsR[:, 0:4, :].rearrange("p g n -> p (g n)"), [128, 512], ["Ws"], BF16)
        dbg_out["CoR"] = (CoR[:, 0:4, :].rearrange("p g n -> p (g n)"), [128, 512], ["CoR"], BF16)
        dbg_out["CoI"] = (CoI[:, 0:4, :].rearrange("p g n -> p (g n)"), [128, 512], ["CoI"], BF16)
        dbg_out["Z0"] = (Z0[:].rearrange("p a g -> p (a g)"), [128, 64], ["Z0"], F32)
        dbg_out["maskT"] = (maskT.rearrange("p h i -> p (h i)"), [128, 512], ["rtab"], F32)

    barrier()
    def main_phases():
        w_in_bf = carve(70, [128, 8, 2560], BF16)
        w_glu_bf = carve(110, [128, 4, 512], BF16)
        mixT = carve(114, [128, 8, 1536], BF16)
        d_win = S.dsem("win", batch=True)
        d_win2 = S.dsem("win2", batch=True)
        w_in_v = w_in.rearrange("(kc p) n -> p kc n", p=128)
        for k2 in range(4):
            dma("pool", w_in_bf[:, 2 * k2:2 * k2 + 2, :], w_in_v[:, 2 * k2:2 * k2 + 2, :], d_win, BARK, ["w_in"])
        dma("pool", w_glu_bf, w_glu.rearrange("(kc p) n -> p kc n", p=128), d_win, BARK, ["w_glu"])
        bgluT = sb("bgluT", [128, 4])
        dma("sp", bgluT[:], b_glu.rearrange("(c p) -> p c", p=128), d_ld1, BARK, ["bgluT"], noncontig=True)

        st6 = sb("st6", [128, 4, 6])
        mv = sb("mv", [128, 4, 2])
        rstd = sb("rstd", [128, 4])
        nmr = sb("nmr", [128, 4])
        Sst = [sb("Sst%d" % d_, [128, 4, 128]) for d_ in range(2)]
        Hmid = sb("Hmid", [128, 2, 32])
        zero64 = sb("zero64", [128, 2, 2, 32])
        memset("dve", zero64[:], 0.0, ["zero64"])
        cp("dve", Hmid[:], Z0[:], ["Z0"], ["Hmid"])
        d_x = [S.dsem("x0"), S.dsem("x1")]
        d_st = S.dsem("st", batch=True)
        rr = [0]

        def halfbank():
            rr[0] = (rr[0] + 1) % 6
            i = rr[0]
            return psD[i // 2][:, (i % 2) * 512:(i % 2) * 512 + 512], "psD%d_%d" % (i // 2, i % 2)

        def ln_stats(src, nchunks, csz, skeys, col=0, tiles=None, tag=None):
            if tiles is None:
                st6_, mv_, rs_, nm_ = st6, mv, rstd, nmr
                sk, mk, rk_, nk_ = "st6_%d" % col, "mv_%d" % col, "rstd_%d" % col, "nmr_%d" % col
                r0 = 2 * col
            else:
                st6_, mv_, rs_, nm_ = tiles
                sk = mk = rk_ = nk_ = tag
                r0 = 0
            for k_ in range(nchunks):
                S.op("dve", lambda e, k_=k_: e.bn_stats(out=st6_[:, r0 + k_, :], in_=src[:, k_ * csz:(k_ + 1) * csz]), skeys, [sk])
            S.op("dve", lambda e: e.bn_aggr(out=mv_[:, col, :], in_=st6_[:, r0:r0 + nchunks, :]), [sk], [mk])
            act(rs_[:, col:col + 1], mv_[:, col, 1:2], AF.Ln, [mk], [rk_], bias=epsc[:, 0:1])
            act(rs_[:, col:col + 1], rs_[:, col:col + 1], AF.Exp, [rk_], [rk_], scale=-0.5)
            stt("dve", nm_[:, col:col + 1], mv_[:, col, 0:1], -1.0, rs_[:, col:col + 1], ALU.mult, ALU.mult, [mk, rk_], [nk_])

        def stage1a(xsrc, ntok, ci, hT, vsh, vsc):
            XIN = [carve(162, [128, 1024]), carve(166, [128, 1024])]
            XN = [carve(170, [128, 1024], BF16), carve(172, [128, 1024], BF16)]
            EV32 = [carve(174, [128, 8, 128]), carve(178, [128, 8, 128])]
            for t in range(ntok // 128):
                b_ = t % 2
                dma("sp", XIN[b_], xsrc[t * 128:(t + 1) * 128, :], d_x[b_], BARK, ["xin%d" % b_])
                ln_stats(XIN[b_], 2, 512, ["xin%d" % b_], col=b_)
                act(XN[b_], XIN[b_], AF.Identity, ["xin%d" % b_, "rstd_%d" % b_, "nmr_%d" % b_], ["xn%d" % b_], bias=nmr[:, b_:b_ + 1],
                    scale=rstd[:, b_:b_ + 1])
                for c in range(8):
                    tr(psT[b_][:, c * 128:(c + 1) * 128], XN[b_][:, c * 128:(c + 1) * 128], ident_b[:], ["xn%d" % b_], ["psT%d" % b_])
                tmp32 = EV32[b_]
                sc_bc = pat(modT[:, vsc, 0, ci:ci + 1], [[2, 8], [0, 128]])
                sh_bc = pat(modT[:, vsh, 0, ci:ci + 1], [[2, 8], [0, 128]])
                tt("dve", tmp32, psT[b_][:].rearrange("p (c n) -> p c n", c=8), sc_bc, ALU.mult, ["psT%d" % b_, "modT"], ["ev32_%d" % b_])
                tt("pool", hT[:, :, t * 128:(t + 1) * 128], tmp32, sh_bc, ALU.add, ["ev32_%d" % b_, "modT"], ["hT"])

        def proj_fm(hT, tg, col0, dst, toggle):
            bank, bk = halfbank()
            for c in range(8):
                mm(bank, w_in_bf[:, c, col0:col0 + 128], hT[:, c, tg * 512:(tg + 1) * 512], c == 0, c == 7, ["w_in", "hT"], [bk])
            cp("act" if toggle else "dve", dst, bank, [bk], ["proj"])

        def proj_tm(hT, t, col0):
            bank, bk = halfbank()
            for c in range(8):
                mm(bank, hT[:, c, t * 128:(t + 1) * 128], w_in_bf[:, c, col0:col0 + 512], c == 0, c == 7, ["w_in", "hT"], [bk])
            return bank, bk

        def proj_u(hT, ntok, u_cm):
            nch = ntok // 8
            for m in range(8):
                bank, bk = halfbank()
                for c in range(8):
                    lhs = pat(hT[:, c, m:m + 1], [[8, nch]])
                    mm(bank[0:nch, :], lhs, w_in_bf[:, c, 2048:2560], c == 0, c == 7, ["w_in", "hT"], [bk])
                cp("act" if m % 2 else "dve", u_cm[0:nch, :, m, :], bank[0:nch, :].rearrange("p (g c) -> p g c", g=32), [bk], ["u_cm"])

        def kv_update(d_, kd, vt, first_keys):
            bank, bk = halfbank()
            for h in range(4):
                mm(bank[:, h * 128:(h + 1) * 128], kd[:, h * 128:(h + 1) * 128], vt[:, h * 128:(h + 1) * 128], True, True,
                   first_keys, [bk])
            tt("pool", Sst[d_][:], Sst[d_][:], dec_tab[:, d_], ALU.mult, ["Sst%d" % d_, "rtab"], ["Sst%d" % d_])
            tt("dve", Sst[d_][:].rearrange("p h e -> p (h e)"), Sst[d_][:].rearrange("p h e -> p (h e)"), bank, ALU.add,
               ["Sst%d" % d_, bk], ["Sst%d" % d_])

        def s5_sums(u_cm, nch, nseq, H, u_arr):
            nk = nch // nseq
            for g8 in range(4):
                b_ = g8 % 2
                for gi in range(8):
                    g_ = g8 * 8 + gi
                    src = u_cm[0:nch, g_].rearrange("p m c -> p (m c)")
                    tr(psT[b_][:, gi * 128:gi * 128 + nch], src, ident_b[0:nch, 0:nch], ["u_cm"], ["psT%d" % b_])
                cp("act" if g8 % 2 else "dve", u_arr[:, g8 * 8:(g8 + 1) * 8, :],
                   psT[b_][:].rearrange("p (g j) -> p g j", g=8)[:, :, 0:nch], ["psT%d" % b_], ["u_arr"])
            for g8 in range(4):
                for ri, Ws in enumerate((WsR, WsI)):
                    pk = "pss%d" % ri
                    for gi in range(8):
                        g_ = g8 * 8 + gi
                        mm(psD[ri][:, gi * nch:(gi + 1) * nch], Ws[:, g_, :], u_arr[:, g_, :], True, True, ["Ws", "u_arr"], [pk])
                    for sq in range(nseq):
                        pin = psD[ri][:, 0:8 * nch].rearrange("p (g s k) -> p g s k", g=8, s=nseq)
                        o_f = pat(H[0:64, sq, 0, ri, g8 * 8:g8 * 8 + 1], [[1, 8], [64, nk]])
                        cp("dve", o_f, pin[0:64, :, sq, :], [pk], ["H"])
                        o_b = pat(H[64:128, sq, 0, ri, g8 * 8:g8 * 8 + 1], [[1, 8], [64, nk]])
                        cp("act", o_b, rev_ap(pin[64:128, :, sq, :], 2), [pk], ["H"])

        def s5_scan(H, nseq, nk, init, lo=0):
            t1 = sb_scan[0][lo:128, 0:nseq]
            t2 = sb_scan[1][lo:128, 0:nseq]
            aab = pat(AA[lo:128, 0, 0:1], [[0, nseq], [1, 64]]).rearrange("p s (r g) -> p s r g", r=2)
            aib = pat(AIs[lo:128, 0, 0:1], [[0, nseq], [1, 64]]).rearrange("p s (r g) -> p s r g", r=2)
            for k_ in range(0 if "noscan" in dbg else (2 if "scan2" in dbg else nk)):
                prev = init[lo:128, 0:nseq] if k_ == 0 else H[lo:128, :, k_ - 1]
                tt("dve", t1, prev, aab, ALU.mult, ["H", "init"], ["sc1"])
                tt("dve", t2, rev_ap(prev, 2), aib, ALU.mult, ["H", "init"], ["sc2"])
                tt("dve", t1, t1, t2, ALU.add, ["sc1", "sc2"], ["sc1"])
                tt("dve", H[lo:128, :, k_], H[lo:128, :, k_], t1, ALU.add, ["H", "sc1"], ["H"])

        def s5_scan_blk(H, nk, init, lo=0, fill=True):
            P_ = slice(lo, 128)
            nb = nk // 8
            nv = nb - 1
            t1b = carve(138, [128, 16, 2, 32])
            t2b = carve(142, [128, 16, 2, 32])
            PAr = carve(146, [128, 8, 32])
            PAi = carve(147, [128, 8, 32])
            AIp = carve(148, [128, 8, 2, 32])
            er = carve(150, [128, 8, 32])
            ei = carve(151, [128, 8, 32])
            mg = carve(152, [128, 8, 32])
            cc = carve(153, [128, 8, 32])
            ft2 = carve(138, [128, 256])
            KB = ["blk_p"]
            E8b = pat(E8[:], [[1, 8], [0, 32]])
            tt("dve", er, E8b, pat(sm["lr"][:], [[0, 8], [1, 32]]), ALU.mult, ["E8", "sm_lr"], KB)
            tt("dve", ei, E8b, pat(sm["li"][:], [[0, 8], [1, 32]]), ALU.mult, ["E8", "sm_li"], KB)
            cplx_pow(PAr, PAi, er, ei, None, KB, KB, (mg, cc, er), ft_=ft2)
            ts("dve", AIp[:, :, 0, :], PAi, -1.0, None, ALU.mult, None, KB, KB)
            cp("dve", AIp[:, :, 1, :], PAi, KB, KB)
            KH = ["H", "init"] + KB

            def blkv(i, b0, n):
                return pat(H[P_, 0, b0 * 8 + i, 0, 0:1], [[512, n], [32, 2], [1, 32]])

            aa1 = pat(AA[P_, 0, 0:1], [[32, 2], [1, 32]])
            ai1 = pat(AIs[P_, 0, 0:1], [[32, 2], [1, 32]])
            ts1, ts2 = t1b[P_, 0], t2b[P_, 0]
            for k_ in range(8):
                prev = init[P_, 0] if k_ == 0 else H[P_, 0, k_ - 1]
                tt("dve", ts1, prev, aa1, ALU.mult, KH, ["sc1"])
                tt("dve", ts2, rev_ap(prev, 1), ai1, ALU.mult, KH, ["sc2"])
                tt("dve", ts1, ts1, ts2, ALU.add, ["sc1", "sc2"], ["sc1"])
                tt("dve", H[P_, 0, k_], H[P_, 0, k_], ts1, ALU.add, ["H", "sc1"], ["H"])
            aaN = pat(AA[P_, 0, 0:1], [[0, nv], [32, 2], [1, 32]])
            aiN = pat(AIs[P_, 0, 0:1], [[0, nv], [32, 2], [1, 32]])
            t1v, t2v = t1b[P_, 0:nv], t2b[P_, 0:nv]
            for i in range(1, 8):
                vp, vi = blkv(i - 1, 1, nv), blkv(i, 1, nv)
                tt("dve", t1v, vp, aaN, ALU.mult, KH, ["sc1"])
                tt("dve", t2v, rev_ap(vp, 2), aiN, ALU.mult, KH, ["sc2"])
                tt("dve", t1v, t1v, t2v, ALU.add, ["sc1", "sc2"], ["sc1"])
                tt("dve", vi, vi, t1v, ALU.add, ["H", "sc1"], ["H"])
            a8r = pat(PAr[P_, 7, 0:1], [[0, 2], [1, 32]])
            a8i = AIp[P_, 7]
            for b in range(1, nb):
                prev = H[P_, 0, (b - 1) * 8 + 7]
                cur = H[P_, 0, b * 8 + 7]
                tt("dve", ts1, prev, a8r, ALU.mult, KH, ["sc1"])
                tt("dve", ts2, rev_ap(prev, 1), a8i, ALU.mult, KH, ["sc2"])
                tt("dve", ts1, ts1, ts2, ALU.add, ["sc1", "sc2"], ["sc1"])
                tt("dve", cur, cur, ts1, ALU.add, ["H", "sc1"], ["H"])
            for i in (range(7) if fill else ()):
                pv, vi = blkv(7, 0, nv), blkv(i, 1, nv)
                ar_ = pat(PAr[P_, i, 0:1], [[0, nv], [0, 2], [1, 32]])
                ai_ = pat(AIp[P_, i, 0, 0:1], [[0, nv], [32, 2], [1, 32]])
                tt("dve", t1v, pv, ar_, ALU.mult, KH, ["sc1"])
                tt("dve", t2v, rev_ap(pv, 2), ai_, ALU.mult, KH, ["sc2"])
                tt("dve", t1v, t1v, t2v, ALU.add, ["sc1", "sc2"], ["sc1"])
                tt("dve", vi, vi, t1v, ALU.add, ["H", "sc1"], ["H"])

        sb_scan = [carve(138, [128, 2, 2, 32]), carve(139, [128, 2, 2, 32])]

        sb_kd = [carve(70, [128, 512], BF16), carve(71, [128, 512], BF16)]
        PMB = [carve(72, [128, 4, 128], BF16), carve(73, [128, 4, 128], BF16)]
        QFB = [carve(74, [128, 4, 128], BF16), carve(75, [128, 4, 128], BF16)]
        QBB = [carve(76, [128, 4, 128], BF16), carve(77, [128, 4, 128], BF16)]
        ONB = [carve(78, [128, 512]), carve(80, [128, 512])]
        OMB = [carve(82, [128, 512], BF16), carve(83, [128, 512], BF16)]
        SIGB = [carve(84, [128, 512], BF16), carve(85, [128, 512], BF16)]
        FINT = [carve(86, [32, 256]), carve(87, [32, 256])]
        FIN = sb("FIN", [128, 2, 32])
        DBGY = carve(90, [128, 4096])
        hT = carve(138, [128, 8, 1024], BF16)
        stage1a(xs[1024:2048, :], 1024, 1, hT, 0, 1)
        if "stop_o1" in dbg:
            dbg_out["hT"] = (hT[:, 0, :], [128, 1024], ["hT"], BF16)
            barrier()
            return
        for h in range(2):
            pass
        dma("sp", Sst[1][:], sret[1].rearrange("h d e -> d h e"), G["st1"], BARK, ["Sst1"])
        dma("sp", Sst[0][:], sret[0].rearrange("h d e -> d h e"), G["st0"], BARK, ["Sst0"])
        barrier()
        u_cm = carve(186, [128, 32, 8, 16], BF16)
        kdt = [carve(154, [128, 512], BF16), carve(155, [128, 512], BF16)]
        vtt = [carve(156, [128, 512], BF16), carve(157, [128, 512], BF16)]
        for t in range(7, -1, -1):
            b_ = t % 2
            bank, bk = proj_tm(hT, t, 512)
            tt("dve", kdt[b_], bank.rearrange("p (h e) -> p h e", h=4), kdb_tab, ALU.mult, [bk, "rtab"], ["kdt%d" % b_]) if False else \
                tt("dve", kdt[b_].rearrange("p (h e) -> p h e", h=4), bank.rearrange("p (h e) -> p h e", h=4), kdb_tab, ALU.mult,
                   [bk, "rtab"], ["kdt%d" % b_])
            bank2, bk2 = proj_tm(hT, t, 1024)
            cp("act", vtt[b_], bank2, [bk2], ["vtt%d" % b_])
            kv_update(1, kdt[b_], vtt[b_], ["kdt%d" % b_, "vtt%d" % b_])
        proj_u(hT, 1024, u_cm)
        barrier()
        if "stop_o2" in dbg:
            dbg_out["Sst1"] = (Sst[1][:].rearrange("p h e -> p (h e)"), [128, 512], ["Sst1"], F32)
            dbg_out["ucm"] = (u_cm.rearrange("p g m c -> p (g m c)"), [128, 4096], ["u_cm"], BF16)
            return
        H = carve(162, [128, 1, 128, 2, 32])
        u_arr = carve(154, [128, 32, 128], BF16)
        s5_sums(u_cm, 128, 1, H, u_arr)
        barrier()
        if "stop_o3" in dbg:
            dbg_out["H"] = (H[:, 0, 0:16].rearrange("p k r g -> p (k r g)"), [128, 1024], ["H"], F32)
            return
        s5_scan_blk(H, 128, Z0[:].rearrange("p (s r) g -> p s r g", s=1), lo=64, fill=False)
        cp("dve", Hmid[64:128], H[64:128, 0, 127], ["H"], ["Hmid"])
        barrier()
        if "stop_o4" in dbg:
            dbg_out["Hmid"] = (Hmid[:].rearrange("p r g -> p (r g)"), [128, 64], ["Hmid"], F32)
            dbg_out["H"] = (H[:, 0, 0:16].rearrange("p k r g -> p (k r g)"), [128, 1024], ["H"], F32)
            return

        X = carve(138, [128, 12, 1024])
        segs = [dict(name="S", xsrc=xs[0:1024, :], ntok=1024, ci=1, nseq=1, moff=0),
                dict(name="P", xsrc=xp, ntok=512, ci=0, nseq=2, moff=1024)]
        for sg_ in segs:
            ntok, ci, nseq, moff = sg_["ntok"], sg_["ci"], sg_["nseq"], sg_["moff"]
            isS = sg_["name"] == "S"
            nt = ntok // 128
            nts = nt // nseq
            nch = ntok // 8
            nk = nch // nseq
            hT = carve(138, [128, 8, ntok], BF16)
            if not isS:
                for k2 in range(4):
                    dma("pool", w_in_bf[:, 2 * k2:2 * k2 + 2, :], w_in_v[:, 2 * k2:2 * k2 + 2, :], d_win2, BARK, ["w_in"])
            stage1a(sg_["xsrc"], ntok, ci, hT, 0, 1)
            barrier()
            qT = carve(154, [128, 4, ntok], BF16)
            kT = carve(162, [128, 4, ntok], BF16)
            vv = carve(170, [128, nt, 512], BF16)
            sgt = carve(178, [128, nt, 512], BF16)
            u_cm = carve(186, [128, 32, 8, 16], BF16)
            for tg in range(ntok // 512):
                for h in range(4):
                    proj_fm(hT, tg, h * 128, qT[:, h, tg * 512:(tg + 1) * 512], h % 2)
                    proj_fm(hT, tg, 512 + h * 128, kT[:, h, tg * 512:(tg + 1) * 512], (h + 1) % 2)
            for t in range(nt):
                bank, bk = proj_tm(hT, t, 1024)
                cp("dve", vv[:, t, :], bank, [bk], ["vv"])
                bank, bk = proj_tm(hT, t, 1536)
                act(sgt[:, t, :], bank, AF.Silu, [bk], ["sgt"])
            proj_u(hT, ntok, u_cm)
            barrier()
            Sbf = carve(138, [128, 2, nt, 4, 128], BF16)
            kdt = [carve(194 - 2, [128, 512], BF16), carve(194 - 1, [128, 512], BF16)] if False else \
                [sb_kd[0][:], sb_kd[1][:]]
            for sq in range(nseq):
                if not isS:
                    memset("dve", Sst[0][:], 0.0, ["Sst0"])
                    memset("pool", Sst[1][:], 0.0, ["Sst1"])
                for idx in range(nts):
                    for d_ in range(2):
                        tl = idx if d_ == 0 else nts - 1 - idx
                        t = sq * nts + tl
                        cp("act", Sbf[:, d_, t], Sst[d_][:], ["Sst%d" % d_], ["Sbf%d" % d_])
                        for h in range(4):
                            tr(psT[d_][:, h * 128:(h + 1) * 128], kT[:, h, t * 128:(t + 1) * 128], ident_b[:], ["proj"], ["psT%d" % d_])
                        tab = kdf_tab if d_ == 0 else kdb_tab
                        tt("dve", kdt[d_].rearrange("p (h e) -> p h e", h=4), psT[d_][:, 0:512].rearrange("p (h e) -> p h e", h=4),
                           tab, ALU.mult, ["psT%d" % d_, "rtab"], ["kdt%d" % d_])
                        kv_update(d_, kdt[d_], vv[:, t, :], ["kdt%d" % d_, "vv"])
                if not isS:
                    for d_ in range(2):
                        dma("sp", nret[sq, d_].rearrange("h d e -> d h e"), Sst[d_][:], d_out, ["Sst%d" % d_], [])
            barrier()
            ST5 = [(carve(88 + 0.25 * k_, [128, 4, 6]), carve(88.125 + 0.25 * k_, [128, 4, 2]),
                    carve(88.1875 + 0.25 * k_, [128, 4]), carve(88.21875 + 0.25 * k_, [128, 4])) for k_ in range(2)]
            for t in range(nt):
                bank, bk = halfbank()
                for h in range(4):
                    mm(bank[:, h * 128:(h + 1) * 128], kT[:, h, t * 128:(t + 1) * 128], qT[:, h, t * 128:(t + 1) * 128], True, True,
                       ["proj"], [bk])
                b_ = t % 2
                Pm, qf, qb, on, om = PMB[b_], QFB[b_], QBB[b_], ONB[b_], OMB[b_]
                tt("dve", Pm[:], bank.rearrange("p (h i) -> p h i", h=4), maskT, ALU.mult, [bk, "rtab"], ["Pm%d" % b_])
                tt("pool", qf[:], qT[:, :, t * 128:(t + 1) * 128], colf, ALU.mult, ["proj", "rtab"], ["qf%d" % b_])
                tt("pool", qb[:], qT[:, :, t * 128:(t + 1) * 128], colb, ALU.mult, ["proj", "rtab"], ["qb%d" % b_])
                po, pk = halfbank()
                for h in range(4):
                    o_ap = po[:, h * 128:(h + 1) * 128]
                    mm(o_ap, Pm[:, h, :], vv[:, t, h * 128:(h + 1) * 128], True, False, ["Pm%d" % b_, "vv"], [pk])
                    mm(o_ap, qf[:, h, :], Sbf[:, 0, t, h, :], False, False, ["qf%d" % b_, "Sbf"], [pk])
                    mm(o_ap, qb[:, h, :], Sbf[:, 1, t, h, :], False, True, ["qb%d" % b_, "Sbf"], [pk])
                st6_, mv_, rs_, nm_ = ST5[b_]
                k5 = "s5st%d" % b_
                for h in range(4):
                    S.op("dve", lambda e, h=h, po=po, st6_=st6_: e.bn_stats(out=st6_[:, h, :], in_=po[:, h * 128:(h + 1) * 128]), [pk], [k5])
                for h in range(4):
                    S.op("dve", lambda e, h=h, st6_=st6_, mv_=mv_: e.bn_aggr(out=mv_[:, h, :], in_=st6_[:, h:h + 1, :]), [k5], [k5])
                act(rs_, mv_[:, :, 1], AF.Ln, [k5], [k5], bias=epsc[:, 0:1])
                act(rs_, rs_, AF.Exp, [k5], [k5], scale=-0.5)
                stt("dve", nm_, mv_[:, :, 0], -1.0, rs_, ALU.mult, ALU.mult, [k5], [k5])
                for h in range(4):
                    act(on[:, h * 128:(h + 1) * 128], po[:, h * 128:(h + 1) * 128], AF.Identity, [pk, k5], ["on%d" % b_],
                        bias=nm_[:, h:h + 1], scale=rs_[:, h:h + 1])
                tt("pool", om[:], on[:], sgt[:, t, :], ALU.mult, ["on%d" % b_, "sgt"], ["om%d" % b_])
                for h in range(4):
                    tr(psT[1][:, h * 128:(h + 1) * 128], om[:, h * 128:(h + 1) * 128], ident_b[:], ["om%d" % b_], ["psT1"])
                cp("act", mixT[:, 0:4, moff + t * 128:moff + (t + 1) * 128], psT[1][:, 0:512].rearrange("p (h i) -> p h i", h=4),
                   ["psT1"], ["mixT"])
            barrier()
            H = carve(162, [128, nseq, nk, 2, 32])
            u_arr = carve(154, [128, 32, nch], BF16)
            s5_sums(u_cm, nch, nseq, H, u_arr)
            barrier()
            init = Hmid[:].rearrange("p (s r) g -> p s r g", s=1) if isS else zero64[:]
            if isS:
                s5_scan_blk(H, nk, init)
            else:
                s5_scan(H, nseq, nk, init)
            barrier()
            Hb = carve(138, [128, 2, 32, nch], BF16)
            for sq in range(nseq):
                j0 = sq * nk
                for lo, hi in ((0, 64), (64, 128)):
                    fwd = lo == 0
                    jinit = j0 if fwd else j0 + nk - 1
                    cp("dve", Hb[lo:hi, :, :, jinit], init[lo:hi, sq], ["init"], ["Hb"])
                    src = pat(H[lo:hi, sq, 0, 0, 0:1], [[32, 2], [1, 32], [64, nk - 1]])
                    if fwd:
                        cp("dve", Hb[lo:hi, :, :, j0 + 1:j0 + nk], src, ["H"], ["Hb"])
                    else:
                        cp("act", Hb[lo:hi, :, :, j0:j0 + nk - 1], rev_ap(src, 3), ["H"], ["Hb"])
            if not isS:
                for sq in range(nseq):
                    fin = FIN
                    last = H[:, sq, nk - 1]
                    tt("dve", fin[:, 0, :], last[:, 0, :], sm["p7r"][:], ALU.mult, ["H"], ["fin"])
                    tt("dve", sm["t1"][:], last[:, 1, :], sm["p7i"][:], ALU.mult, ["H", "fin"], ["sm_t1"])
                    tt("dve", fin[:, 0, :], fin[:, 0, :], sm["t1"][:], ALU.subtract, ["fin", "sm_t1"], ["fin"])
                    tt("dve", fin[:, 1, :], last[:, 0, :], sm["p7i"][:], ALU.mult, ["H"], ["fin"])
                    tt("dve", sm["t1"][:], last[:, 1, :], sm["p7r"][:], ALU.mult, ["H", "fin"], ["sm_t1"])
                    tt("dve", fin[:, 1, :], fin[:, 1, :], sm["t1"][:], ALU.add, ["fin", "sm_t1"], ["fin"])
                    bank, bk = halfbank()
                    for ri in range(2):
                        mm(bank[0:32, ri * 128:(ri + 1) * 128], fin[:, ri, :], ident_f[:], True, True, ["fin"], [bk])
                    cp("dve", FINT[sq][:], bank[0:32, 0:256], [bk], ["fint%d" % sq])
                    for ri, dst in enumerate((ns5r, ns5i)):
                        dma("sp", dst[sq].rearrange("d g p -> g d p"), FINT[sq][:, ri * 128:(ri + 1) * 128].rearrange("g (d p) -> g d p", d=2),
                            d_out, ["fint%d" % sq], [])
            y2cm = carve(162, [128, 8, 32, 16], BF16)
            y2T = carve(170, [128, 4, ntok], BF16)
            barrier()
            for g8 in range(4):
                po = psD[g8 % 3]
                pk = "psy%d" % (g8 % 3)
                for gi in range(8):
                    g_ = g8 * 8 + gi
                    o_ap = po[0:nch, gi * 128:(gi + 1) * 128]
                    mm(o_ap, u_arr[:, g_, :], Mtoe[:, g_, :], True, False, ["u_arr", "Mtoe"], [pk])
                    mm(o_ap, Hb[:, 0, g_, :], CoR[:, g_, :], False, False, ["Hb", "CoR"], [pk])
                    mm(o_ap, Hb[:, 1, g_, :], CoI[:, g_, :], False, True, ["Hb", "CoI"], [pk])
                act(y2cm[0:nch, :, g8 * 8:(g8 + 1) * 8, :].rearrange("p t g c -> p g t c"), po[0:nch, :].rearrange("p (g t c) -> p g t c", g=8, t=8),
                    AF.Gelu_apprx_tanh, [pk], ["y2cm"])
                if "s5y" in dbg and not isS:
                    cp("dve", DBGY[0:nch, g8 * 1024:(g8 + 1) * 1024], po[0:nch, :], [pk], ["dbgy"])
            for fb in range(4):
                b_ = fb % 2
                for t8 in range(8):
                    src = y2cm[0:nch, t8, fb * 8:(fb + 1) * 8, :].rearrange("p g c -> p (g c)")
                    tr(psT[b_][:, t8 * 128:t8 * 128 + nch], src, ident_b[0:nch, 0:nch], ["y2cm"], ["psT%d" % b_])
                o_ap = pat(y2T[:, fb, 0:1], [[8, nch], [1, 8]])
                i_ap = pat(psT[b_][:, 0:1], [[1, nch], [128, 8]])
                cp("act" if fb % 2 else "dve", o_ap, i_ap, ["psT%d" % b_], ["y2T"])
            for n_ in range(4):
                for tg in range(ntok // 512):
                    bank, bk = halfbank()
                    for fb in range(4):
                        mm(bank, w_glu_bf[:, fb, n_ * 128:(n_ + 1) * 128], y2T[:, fb, tg * 512:(tg + 1) * 512], fb == 0, fb == 3,
                           ["w_glu", "y2T"], [bk])
                    b_ = (n_ + tg) % 2
                    act(SIGB[b_][:], bank, AF.Sigmoid, [bk, "bgluT"], ["sig%d" % b_], bias=bgluT[:, n_:n_ + 1])
                    tt("pool", mixT[:, 4 + n_, moff + tg * 512:moff + (tg + 1) * 512], y2T[:, n_, tg * 512:(tg + 1) * 512], SIGB[b_][:],
                       ALU.mult, ["y2T", "sig%d" % b_], ["mixT"])
            barrier()
        if "s5y" in dbg:
            dbg_out["y"] = (DBGY[0:64, :], [64, 4096], ["dbgy"], F32)
        if "mix" in dbg:
            dbg_out["mixT"] = (mixT[:, :, 1024:1536].rearrange("p k n -> p (k n)"), [128, 4096], ["mixT"], BF16)

        w_out_bf = carve(70, [128, 8, 1024], BF16)
        ln1g_bc = carve(86, [128, 1024])
        ln1b_bc = carve(90, [128, 1024])
        XIN2 = [carve(94 + 4 * k_, [128, 1024]) for k_ in range(4)]
        RT = [carve(32 + 4 * k_, [128, 1024]) for k_ in range(4)]
        C2ST = [(carve(48 + 0.25 * k_, [128, 4, 6]), carve(48.125 + 0.25 * k_, [128, 4, 2]),
                 carve(48.1875 + 0.25 * k_, [128, 4]), carve(48.21875 + 0.25 * k_, [128, 4])) for k_ in range(4)]
        d_x4 = d_x + [S.dsem("x2"), S.dsem("x3")]
        d_wo = S.dsem("wo")
        dma("pool", w_out_bf, w_out.rearrange("(kc p) n -> p kc n", p=128), d_wo, BARK, ["w_out"])
        dma("sp", ln1g_bc, rowbc(ln1_g, 128), d_ld2, BARK, ["ln1g"])
        dma("sp", ln1b_bc, rowbc(ln1_b, 128), d_ld2, BARK, ["ln1b"])
        W1s = [carve(0, [128, 8, 1024], BF16), carve(32, [128, 8, 1024], BF16)]
        W2s = [carve(16, [128, 8, 1024], BF16), carve(70, [128, 8, 1024], BF16)]
        d_ffa = [S.dsem("ffa0"), S.dsem("ffa1")]
        d_ffb = [S.dsem("ffb0"), S.dsem("ffb1")]
        w_ff1_v = w_ff1.rearrange("(kc p) n -> p kc n", p=128)
        w_ff2_v = w_ff2.rearrange("(q fc p) n -> q p fc n", q=4, p=128)

        def load_ff(q_):
            s_ = q_ % 2
            dma("pool", W1s[s_], w_ff1_v[:, :, q_ * 1024:(q_ + 1) * 1024], d_ffa[s_], BARK, ["W1_%d" % s_])
            dma("pool", W2s[s_], w_ff2_v[q_], d_ffb[s_], BARK, ["W2_%d" % s_])

        load_ff(0)

        def own_x(t):
            return (xs[t * 128:(t + 1) * 128, :], 1) if t < 8 else (xp[(t - 8) * 128:(t - 7) * 128, :], 0)

        for t in range(12):
            b_ = t % 4
            src, ci = own_x(t)
            dma("sp", XIN2[b_], src, d_x4[b_], BARK, ["xin2_%d" % b_])
            po = psD[t % 3][:]
            pk = "pso%d" % (t % 3)
            for hh in range(2):
                for kc in range(8):
                    mm(po[:, hh * 512:(hh + 1) * 512], mixT[:, kc, t * 128:(t + 1) * 128], w_out_bf[:, kc, hh * 512:(hh + 1) * 512],
                       kc == 0, kc == 7, ["mixT", "w_out"], [pk])
            r_ = RT[b_]
            tt("dve", r_, po, g1bc[:, ci, :], ALU.mult, [pk, "gbc"], ["rt%d" % b_])
            stt("dve", r_, XIN2[b_], ALPHA, r_, ALU.mult, ALU.add, ["xin2_%d" % b_, "rt%d" % b_], ["rt%d" % b_])
            tl_ = C2ST[b_]
            kst = "c2st%d" % b_
            ln_stats(r_, 2, 512, ["rt%d" % b_], col=0, tiles=tl_, tag=kst)
            act(X[:, t, :], r_, AF.Identity, ["rt%d" % b_, kst], ["X%d" % t], bias=tl_[3][:, 0:1], scale=tl_[2][:, 0:1])
            tt("pool", X[:, t, :], X[:, t, :], ln1g_bc, ALU.mult, ["X%d" % t, "ln1g"], ["X%d" % t])
            tt("pool", X[:, t, :], X[:, t, :], ln1b_bc, ALU.add, ["X%d" % t, "ln1b"], ["X%d" % t])
        if "x1" in dbg:
            dbg_out["x1"] = (X[:, 8, :], [128, 1024], ["X8"], F32)
        barrier()

        load_ff(1)
        h2T = carve(114, [128, 8, 1536], BF16)
        ln2g_bc = carve(102, [128, 1024])
        ln2b_bc = carve(106, [128, 1024])
        gb2 = carve(186, [128, 2, 1024])
        TT = [carve(48, [128, 1024]), carve(110, [128, 1024])]
        FT = [carve(86, [128, 8, 512], BF16), carve(94, [128, 8, 512], BF16)]
        bff1T = sb("bff1T", [128, 32])
        RELB = [carve(52, [128, 512], BF16), carve(53, [128, 512], BF16)]
        dma("sp", ln2g_bc, rowbc(ln2_g, 128), d_ld3, BARK, ["ln2g"])
        dma("sp", ln2b_bc, rowbc(ln2_b, 128), d_ld3, BARK, ["ln2b"])
        dma("sp", bff1T[:], b_ff1.rearrange("(c p) -> p c", p=128), G["ld4"], BARK, ["bff1T"], noncontig=True)
        for ci in range(2):
            dma("sp", gb2[:, ci, :], rowbc(b_ff2, 128), G["ld5"], BARK, ["gb2"])
        tt("dve", gb2, gb2, g2bc, ALU.mult, ["gb2", "gbc"], ["gb2"])
        XN2 = [carve(98, [128, 1024], BF16), carve(100, [128, 1024], BF16)]
        EV32D = [carve(86, [128, 8, 128]), carve(90, [128, 8, 128])]
        for t in range(12):
            b_ = t % 2
            ci = 1 if t < 8 else 0
            ln_stats(X[:, t, :], 2, 512, ["X%d" % t], col=b_)
            act(XN2[b_], X[:, t, :], AF.Identity, ["X%d" % t, "rstd_%d" % b_, "nmr_%d" % b_], ["xn%d" % b_], bias=nmr[:, b_:b_ + 1],
                scale=rstd[:, b_:b_ + 1])
            for c in range(8):
                tr(psT[b_][:, c * 128:(c + 1) * 128], XN2[b_][:, c * 128:(c + 1) * 128], ident_b[:], ["xn%d" % b_], ["psT%d" % b_])
            tmp32 = EV32D[b_]
            sc_bc = pat(modT[:, 3, 0, ci:ci + 1], [[2, 8], [0, 128]])
            sh_bc = pat(modT[:, 2, 0, ci:ci + 1], [[2, 8], [0, 128]])
            tt("dve", tmp32, psT[b_][:].rearrange("p (c n) -> p c n", c=8), sc_bc, ALU.mult, ["psT%d" % b_, "modT"], ["ev32_%d" % b_])
            tt("pool", h2T[:, :, t * 128:(t + 1) * 128], tmp32, sh_bc, ALU.add, ["ev32_%d" % b_, "modT"], ["h2T"])
            stt("dve", X[:, t, :], X[:, t, :], ALPHA, gb2[:, ci, :], ALU.mult, ALU.add, ["X%d" % t, "gb2"], ["X%d" % t])
        ffb = [psT[0][:].bitcast(F32), psT[1][:].bitcast(F32), psD[0][:, 0:512], psD[0][:, 512:1024]]
        ffk = ["psT0", "psT1", "psD0_0", "psD0_1"]
        ffi = [0]
        for q_ in range(4):
            s_ = q_ % 2
            if 1 <= q_ <= 2:
                load_ff(q_ + 1)
            for tg in range(3):
                fb_ = (q_ * 3 + tg) % 2
                ft = FT[fb_]
                for fc in range(8):
                    ffi[0] = (ffi[0] + 1) % 4
                    bank, bk = ffb[ffi[0]], ffk[ffi[0]]
                    for kc in range(8):
                        mm(bank, W1s[s_][:, kc, fc * 128:(fc + 1) * 128], h2T[:, kc, tg * 512:(tg + 1) * 512], kc == 0, kc == 7,
                           ["W1_%d" % s_, "h2T"], [bk])
                    act(RELB[fc % 2], bank, AF.Relu, [bk, "bff1T"], ["relb%d" % (fc % 2)],
                        bias=bff1T[:, q_ * 8 + fc:q_ * 8 + fc + 1])
                    tt("pool", ft[:, fc, :], RELB[fc % 2], RELB[fc % 2], ALU.mult, ["relb%d" % (fc % 2)], ["ft%d_%d" % (fb_, fc)])
                for tl in range(4):
                    t = tg * 4 + tl
                    ci = 1 if t < 8 else 0
                    po = psD[1 + (t % 2)][:]
                    pk = "pso%d" % (1 + (t % 2))
                    for hh in range(2):
                        for fc in range(8):
                            mm(po[:, hh * 512:(hh + 1) * 512], ft[:, fc, tl * 128:(tl + 1) * 128], W2s[s_][:, fc, hh * 512:(hh + 1) * 512],
                               fc == 0, fc == 7, ["ft%d_%d" % (fb_, fc), "W2_%d" % s_], [pk])
                    tb_ = TT[t % 2]
                    tt("dve", tb_, po, g2bc[:, ci, :], ALU.mult, [pk, "gbc"], ["tt%d" % (t % 2)])
                    tt("pool", X[:, t, :], X[:, t, :], tb_, ALU.add, ["X%d" % t, "tt%d" % (t % 2)], ["X%d" % t])
        for t in range(12):
            c2_ = t % 2
            ln_stats(X[:, t, :], 2, 512, ["X%d" % t], col=c2_)
            tb_ = TT[t % 2]
            act(tb_, X[:, t, :], AF.Identity, ["X%d" % t, "rstd_%d" % c2_, "nmr_%d" % c2_], ["tt%d" % (t % 2)], bias=nmr[:, c2_:c2_ + 1],
                scale=rstd[:, c2_:c2_ + 1])
            tt("pool", tb_, tb_, ln2g_bc, ALU.mult, ["tt%d" % (t % 2), "ln2g"], ["tt%d" % (t % 2)])
            tt("pool", tb_, tb_, ln2b_bc, ALU.add, ["tt%d" % (t % 2), "ln2b"], ["tt%d" % (t % 2)])
            dst = ys[t * 128:(t + 1) * 128, :] if t < 8 else yp[(t - 8) * 128:(t - 7) * 128, :]
            dma("sp", dst, tb_, d_oy[t % 2], ["tt%d" % (t % 2)], [])


    if "stop_setup" not in dbg:
        main_phases()

    return finish()


def make_in_maps(inp):
    g = lambda k: np.ascontiguousarray(np.asarray(inp[k], dtype=np.float32))
    x_prompt, x_sample = g("x_prompt"), g("x_sample")
    state_ret, s5r, s5i = g("state_ret"), g("state_s5_re"), g("state_s5_im")
    c, c_ctx = g("c"), g("c_ctx")
    maps = []
    for core in range(8):
        b, half = core // 2, core % 2
        flip = half == 1
        dsl = slice(None, None, -1) if flip else slice(None)
        xs_full = x_sample[b]
        xs_v = xs_full[::-1] if flip else xs_full
        xp_v = x_prompt[2 * core:2 * core + 2]
        if flip:
            xp_v = xp_v[:, ::-1]
        m = {
            "xs": np.ascontiguousarray(xs_v),
            "xp": np.ascontiguousarray(xp_v.reshape(512, 1024)),
            "cond": np.ascontiguousarray(np.stack([c_ctx, c[b]])),
            "sret": np.ascontiguousarray(state_ret[b, 0][dsl]),
            "ss5r": np.ascontiguousarray(s5r[b, 0][dsl]),
            "ss5i": np.ascontiguousarray(s5i[b, 0][dsl]),
            "w_ada": g("w_ada")[0], "b_ada": g("b_ada")[0].reshape(1, 6144), "w_in": g("w_in")[0],
            "ret_decay": np.ascontiguousarray(g("ret_decay")[0][dsl]).reshape(1, 8),
            "a_re": np.ascontiguousarray(g("s5_a_re")[0][dsl]),
            "a_im": np.ascontiguousarray(g("s5_a_im")[0][dsl]),
            "log_dt": np.ascontiguousarray(g("s5_log_dt")[0][dsl]),
            "b_re": g("s5_b_re")[0], "b_im": g("s5_b_im")[0], "c_re": g("s5_c_re")[0], "c_im": g("s5_c_im")[0],
            "s5_d": g("s5_d")[0].reshape(1, 512), "w_glu": g("w_glu")[0], "b_glu": g("b_glu")[0],
            "w_out": g("w_out")[0], "ln1_g": g("ln1_g")[0].reshape(1, 1024), "ln1_b": g("ln1_b")[0].reshape(1, 1024),
            "w_ff1": g("w_ff1")[0], "b_ff1": g("b_ff1")[0], "w_ff2": g("w_ff2")[0],
            "b_ff2": g("b_ff2")[0].reshape(1, 1024),
            "ln2_g": g("ln2_g")[0].reshape(1, 1024), "ln2_b": g("ln2_b")[0].reshape(1, 1024),
        }
        maps.append(m)
    return maps


def kernel(**inp):
    nc, _ = build()
    maps = make_in_maps(inp)
    res = run_bass_kernel_spmd(nc, maps, core_ids=list(range(8)))
    y_p = np.zeros((16, 256, 1024), np.float32)
    y_s = np.zeros((4, 2048, 1024), np.float32)
    n_ret = np.zeros((16, 1, 2, 4, 128, 128), np.float32)
    n_s5r = np.zeros((16, 1, 2, 32, 64), np.float32)
    n_s5i = np.zeros((16, 1, 2, 32, 64), np.float32)
    for core in range(8):
        r = res.results[core]
        b, half = core // 2, core % 2
        flip = half == 1
        dsl = slice(None, None, -1) if flip else slice(None)
        ypc = r["yp"].reshape(2, 256, 1024)
        ysc = r["ys"]
        if flip:
            ypc = ypc[:, ::-1]
            y_s[b, 1024:] = ysc[::-1]
        else:
            y_s[b, :1024] = ysc
        y_p[2 * core:2 * core + 2] = ypc
        n_ret[2 * core:2 * core + 2, 0] = r["nret"][:, dsl]
        n_s5r[2 * core:2 * core + 2, 0] = r["ns5r"][:, dsl]
        n_s5i[2 * core:2 * core + 2, 0] = r["ns5i"][:, dsl]
    return (y_p, y_s, n_ret, n_s5r, n_s5i)
```

```python
import numpy as np
from contextlib import ExitStack
import concourse.bass as bass
import concourse.mybir as mybir
from concourse.bass_utils import run_bass_kernel_spmd

F32 = mybir.dt.float32
BF16 = mybir.dt.bfloat16
I32 = mybir.dt.int32
ALU = mybir.AluOpType
AF = mybir.ActivationFunctionType
PI = float(np.pi)
ALPHA = 2.0 ** 0.25
EPS = 1e-5
ENGS = ["pe", "act", "dve", "pool", "sp"]


class Op:
    __slots__ = ("eng", "fn", "deps", "ticket", "signal", "is_dma", "dsem", "dtarget", "hard", "raw", "noinst")


class DSem:
    def __init__(self, sem, batch=False):
        self.sem = sem
        self.count = 0
        self.batch = batch


class Sched:
    def __init__(self, nc, stack):
        self.nc = nc
        self.stack = stack
        self.ops = {e: [] for e in ENGS}
        self.lastw = {}
        self.readers = {}
        self.esem = {e: stack.enter_context(nc.semaphore("s_" + e)) for e in ENGS}
        self.dsems = []
        self.alias = {}

    def expand(self, keys):
        out = []
        for k in keys:
            out.extend(self.alias.get(k, (k,)))
        return out

    def dsem(self, name, batch=False):
        d = DSem(self.stack.enter_context(self.nc.semaphore("d_" + name)), batch)
        self.dsems.append(d)
        return d

    def op(self, eng, fn, r=(), w=(), dsem=None, hard=False):
        o = Op()
        o.hard = hard
        o.noinst = False
        o.eng, o.fn, o.is_dma, o.dsem = eng, fn, dsem is not None, dsem
        o.signal, o.ticket, o.dtarget = False, None, None
        if dsem is not None:
            dsem.count += 16
            o.dtarget = dsem.count
        deps = set()
        raw = set()
        r = self.expand(r)
        w = self.expand(w)
        for b in r:
            lw = self.lastw.get(b)
            if lw is not None:
                deps.add(lw)
                raw.add(lw)
            self.readers.setdefault(b, []).append(o)
        for b in w:
            lw = self.lastw.get(b)
            if lw is not None:
                deps.add(lw)
            for rd in self.readers.get(b, ()):
                deps.add(rd)
            self.lastw[b] = o
            self.readers[b] = []
        deps.discard(o)
        raw.discard(o)
        o.deps = deps
        o.raw = raw
        self.ops[eng].append(o)
        return o

    def finalize(self):
        for e in ENGS:
            for o in self.ops[e]:
                for d in o.deps:
                    if not d.is_dma and (d.eng != o.eng or o.is_dma or o.hard or (o.eng != "pe")):
                        d.signal = True
        for e in ENGS:
            n = 0
            for o in self.ops[e]:
                if o.signal and not o.is_dma:
                    n += 1
                    o.ticket = n

    def emit(self, eng, eh, final_waits=()):
        waited = {}
        for o in self.ops[eng]:
            need = {}
            for d in o.deps:
                if d.is_dma:
                    if o.is_dma and d.dsem is o.dsem and d.dsem.batch:
                        continue
                    sem, val = d.dsem.sem, (d.dsem.count if d.dsem.batch else d.dtarget)
                elif d.eng != eng or o.is_dma or o.hard or (eng != "pe"):
                    sem, val = self.esem[d.eng], d.ticket
                else:
                    continue
                k = id(sem)
                if need.get(k, (None, 0))[1] < val:
                    need[k] = (sem, val)
            if o.is_dma and not o.dsem.batch and o.dtarget > 16:
                k = id(o.dsem.sem)
                if need.get(k, (None, 0))[1] < o.dtarget - 16:
                    need[k] = (o.dsem.sem, o.dtarget - 16)
            for k, (sem, val) in need.items():
                if waited.get(k, 0) < val:
                    eh.wait_ge(sem, val)
                    waited[k] = val
            ins = o.fn(eh)
            if ins is None:
                continue
            if o.is_dma:
                ins.then_inc(o.dsem.sem, 16)
            elif o.signal:
                ins.then_inc(self.esem[eng], 1)
        for d in final_waits:
            if d.count > 0:
                eh.wait_ge(d.sem, d.count)


def rev_ap(ap, dim):
    pat = [list(x) for x in ap.ap]
    st, n = pat[dim]
    off = ap.offset + st * (n - 1)
    pat[dim] = [-st, n]
    return bass.AP(ap.tensor, off, pat)


SKIP_BARRIERS = {2, 3, 4, 6, 7, 8, 9, 10, 12, 14, 15, 16, 17, 18, 20, 23}


def build(dbg=()):
    nc = bass.Bass("TRN2", target_bir_lowering=False)
    stack = ExitStack()
    S = Sched(nc, stack)
    dbg_out = {}

    def din(name, shape):
        return nc.dram_tensor(name, list(shape), F32, kind="ExternalInput").ap()

    def dout(name, shape):
        return nc.dram_tensor(name, list(shape), F32, kind="ExternalOutput").ap()

    xs = din("xs", [2048, 1024])
    xp = din("xp", [512, 1024])
    cond = din("cond", [2, 1024])
    sret = din("sret", [2, 4, 128, 128])
    ss5r = din("ss5r", [2, 32, 64])
    ss5i = din("ss5i", [2, 32, 64])
    w_ada = din("w_ada", [1024, 6144])
    b_ada = din("b_ada", [1, 6144])
    w_in = din("w_in", [1024, 2560])
    ret_decay = din("ret_decay", [1, 8])
    a_re = din("a_re", [2, 32, 64])
    a_im = din("a_im", [2, 32, 64])
    log_dt = din("log_dt", [2, 32])
    b_re = din("b_re", [32, 64, 16])
    b_im = din("b_im", [32, 64, 16])
    c_re = din("c_re", [32, 16, 64])
    c_im = din("c_im", [32, 16, 64])
    s5_d = din("s5_d", [1, 512])
    w_glu = din("w_glu", [512, 512])
    b_glu = din("b_glu", [512])
    w_out = din("w_out", [1024, 1024])
    ln1_g = din("ln1_g", [1, 1024])
    ln1_b = din("ln1_b", [1, 1024])
    w_ff1 = din("w_ff1", [1024, 4096])
    b_ff1 = din("b_ff1", [4096])
    w_ff2 = din("w_ff2", [4096, 1024])
    b_ff2 = din("b_ff2", [1, 1024])
    ln2_g = din("ln2_g", [1, 1024])
    ln2_b = din("ln2_b", [1, 1024])

    yp = dout("yp", [512, 1024])
    ys = dout("ys", [1024, 1024])
    nret = dout("nret", [2, 2, 4, 128, 128])
    ns5r = dout("ns5r", [2, 2, 32, 64])
    ns5i = dout("ns5i", [2, 2, 32, 64])

    def sb(name, shape, dt=F32):
        return stack.enter_context(nc.sbuf_tensor(name, list(shape), dt))

    def ps(name, shape, dt=F32):
        return stack.enter_context(nc.psum_tensor(name, list(shape), dt))

    psT = [ps("psT%d" % i, [128, 1024], BF16) for i in range(2)]
    psD = [ps("psD%d" % i, [128, 1024], F32) for i in range(3)]

    for i_ in range(3):
        for nm_ in ("psy", "pso", "pss", "psfull"):
            S.alias["%s%d" % (nm_, i_)] = ["psD%d_0" % i_, "psD%d_1" % i_]
    for nm_ in ("st6", "mv", "rstd", "nmr"):
        S.alias[nm_] = [nm_ + "_0", nm_ + "_1"]
    S.alias["Sbf"] = ["Sbf0", "Sbf1"]
    for nm_, i_ in (("psmod", 0), ("psg0", 1), ("psg1", 2), ("pss5a", 0), ("pss5c0", 1), ("pss5c1", 2)):
        S.alias[nm_] = ["psD%d_0" % i_, "psD%d_1" % i_]
    def tt(eng, out, a, b, op, r, w):
        return S.op(eng, lambda e: e.tensor_tensor(out=out, in0=a, in1=b, op=op), r, w)

    def ts(eng, out, a, s1, s2, op0, op1, r, w):
        hard = not isinstance(s1, (int, float)) or not (s2 is None or isinstance(s2, (int, float)))
        if s2 is None:
            return S.op(eng, lambda e: e.tensor_scalar(out=out, in0=a, scalar1=s1, scalar2=None, op0=op0), r, w, hard=hard)
        return S.op(eng, lambda e: e.tensor_scalar(out=out, in0=a, scalar1=s1, scalar2=s2, op0=op0, op1=op1), r, w, hard=hard)

    def stt(eng, out, a, sc, b, op0, op1, r, w):
        return S.op(eng, lambda e: e.scalar_tensor_tensor(out=out, in0=a, scalar=sc, in1=b, op0=op0, op1=op1), r, w,
                    hard=not isinstance(sc, (int, float)))

    def cp(eng, out, a, r, w):
        if eng == "act":
            return S.op(eng, lambda e: e.activation(out=out, in_=a, func=AF.Identity), r, w)
        return S.op(eng, lambda e: e.tensor_copy(out=out, in_=a), r, w)

    def act(out, a, func, r, w, bias=None, scale=None):
        kw = {}
        if bias is not None:
            kw["bias"] = bias
        if scale is not None:
            kw["scale"] = scale
        hard = any(not isinstance(v_, (int, float)) for v_ in kw.values())
        return S.op("act", lambda e: e.activation(out=out, in_=a, func=func, **kw), r, w, hard=hard)

    def mm(out, lhsT, rhs, start, stop, r, w):
        return S.op("pe", lambda e: e.matmul(out, lhsT=lhsT, rhs=rhs, start=start, stop=stop), r, w)

    def tr(out, a, ident, r, w):
        return S.op("pe", lambda e: e.transpose(out=out, in_=a, identity=ident), r, w)

    def dma(q, out, a, dsem, r, w, noncontig=False):
        if noncontig:
            return S.op(q, lambda e: e.dma_start(out=out, in_=a, allow_slow_non_contiguous=True), r, w, dsem=dsem)
        return S.op(q, lambda e: e.dma_start(out=out, in_=a), r, w, dsem=dsem)

    def memset(eng, ap, val, w):
        return S.op(eng, lambda e: e.memset(ap, val), (), w)

    d_out = S.dsem("out")
    d_ld0 = S.dsem("ld0", batch=True)
    G = {nm_: S.dsem("g_" + nm_, batch=True) for nm_ in ("cond", "bada", "rd", "ain", "ldt", "bt", "cin", "dbc", "ld4", "ld5", "st0", "st1")}
    d_ld1 = S.dsem("ld1", batch=True)
    d_ld2 = S.dsem("ld2", batch=True)
    d_ld3 = S.dsem("ld3", batch=True)
    d_oy = [S.dsem("oy0"), S.dsem("oy1")]

    def finish():
        dbg_dram = {}
        for name, ent in dbg_out.items():
            ap, shape, keys = ent[0], ent[1], ent[2]
            dbg_dram[name] = dout("dbg_" + name, shape)
            q_ = "pool" if (len(ent) > 3 and ent[3] == BF16) else "sp"
            dma(q_, dbg_dram[name][:, :], ap, d_out, list(keys) + ["BAR"], [])

        S.finalize()
        with nc.Block() as block:
            @block.sync
            def _(e):
                S.emit("sp", e, final_waits=S.dsems)

            @block.scalar
            def _(e):
                S.emit("act", e)

            @block.vector
            def _(e):
                S.emit("dve", e)

            @block.gpsimd
            def _(e):
                S.emit("pool", e)

            @block.tensor
            def _(e):
                S.emit("pe", e)
        stack.close()
        return nc, list(dbg_dram.keys())

    AKB = 194
    arena = sb("arena", [128, AKB * 256], F32)

    def carve(off_kb, shape, dt=F32, p0=0):
        n = int(np.prod(shape[1:]))
        words = n if dt == F32 else (n + 1) // 2
        w0 = int(round(off_kb * 256))
        assert w0 + words <= AKB * 256, (off_kb, shape)
        ap = arena[p0:p0 + shape[0], w0:w0 + words]
        if dt != F32:
            ap = ap.bitcast(dt)
        if len(shape) > 2:
            names = " ".join("d%d" % i for i in range(1, len(shape)))
            ap = ap.rearrange("p (%s) -> p %s" % (names, names), **{"d%d" % i: shape[i] for i in range(1, len(shape) - 1)})
        return ap

    def pat(ap, dims, off=0):
        return bass.AP(ap.tensor, ap.offset + off, [list(ap.ap[0])] + [list(d) for d in dims])

    def rowbc(ap_row, P):
        pt = [list(x) for x in ap_row.ap]
        if len(pt) == 2:
            pt = pt[1:]
        return bass.AP(ap_row.tensor, ap_row.offset, [[0, P]] + pt)

    ident_f = sb("ident_f", [128, 128], F32)
    ident_b = sb("ident_b", [128, 128], BF16)
    ones_f = sb("ones_f", [128, 128], F32)
    iota_i = sb("iota_i", [128, 128], I32)
    ii = sb("ii", [128, 128], F32)
    jj = sb("jj", [128, 1], F32)
    dmat = sb("dmat", [128, 128], F32)
    bar_t = sb("bar_t", [128, 8], F32)
    epsc = sb("epsc", [128, 1], F32)
    itmp = sb("itmp", [128, 256], I32)
    ftmp_holder = [None]

    S.op("pool", lambda e: e.iota(iota_i[:], pattern=[[1, 128]], base=0, channel_multiplier=0), (), ["iota_i"])
    cp("dve", ii[:], iota_i[:], ["iota_i"], ["ii"])
    S.op("pool", lambda e: e.iota(iota_i[:, 0:1], pattern=[[0, 1]], base=0, channel_multiplier=1), ["iota_i"], ["iota_i"])
    cp("dve", jj[:], iota_i[:, 0:1], ["iota_i"], ["jj"])
    ts("dve", dmat[:], ii[:], jj[:, 0:1], None, ALU.subtract, None, ["ii", "jj"], ["dmat"])
    ts("dve", ident_f[:], dmat[:], 0.0, None, ALU.is_equal, None, ["dmat"], ["ident_f"])
    cp("dve", ident_b[:], ident_f[:], ["ident_f"], ["ident_b"])
    memset("dve", ones_f[:], 1.0, ["ones_f"])
    memset("dve", epsc[:], EPS, ["epsc"])

    if "stop_const" in dbg:
        dbg_out["ii"] = (ii[:], [128, 128], ["ii"], F32)
        dbg_out["jj"] = (jj[:], [128, 1], ["jj"], F32)
        dbg_out["dmat"] = (dmat[:], [128, 128], ["dmat"], F32)
        dbg_out["ident"] = (ident_f[:], [128, 128], ["ident_f"], F32)
        return finish()
    bar_n = [0]

    def barrier():
        bar_n[0] += 1
        if bar_n[0] in SKIP_BARRIERS:
            return
        a_ops = []
        for e_ in ("pe", "act", "dve", "pool"):
            for o_ in reversed(S.ops[e_]):
                if not o_.noinst:
                    a_ops.append(o_)
                    break
        last_dma = {}
        for q in ENGS:
            for o in S.ops[q]:
                if o.is_dma:
                    last_dma[id(o.dsem)] = o
        b1 = S.op("dve", lambda e: e.memset(bar_t[:, 0:1], 0.0), (), ["BAR1"])
        b1.deps |= set(a_ops) | set(last_dma.values())
        b1.deps.discard(b1)
        b1.hard = True
        b = S.op("act", lambda e: e.activation(out=bar_t[:, 1:2], in_=ones_f[:, 0:1], func=AF.Identity), ["BAR1"], ["BAR"])
        b.deps |= set(a_ops)
        b.deps.discard(b)
        b.hard = True
        S.op("pool", lambda e: e.memset(bar_t[:, 2:3], 0.0), ["BAR"], [])
        S.op("dve", lambda e: e.memset(bar_t[:, 3:4], 0.0), ["BAR"], [])
        S.op("pe", lambda e: None, ["BAR"], []).noinst = True
        S.lastw.clear()
        S.readers.clear()
        S.lastw["BAR"] = b

    BARK = ["BAR"]

    maskT = carve(0, [128, 4, 128])
    colf = carve(2, [128, 4, 128])
    colb = carve(4, [128, 4, 128])
    kdf_tab = carve(6, [128, 4, 128])
    kdb_tab = carve(8, [128, 4, 128])
    dec_tab = carve(10, [128, 2, 4, 128])
    Mtoe = carve(14, [128, 32, 128], BF16)
    WsR = carve(22, [128, 32, 128], BF16)
    WsI = carve(30, [128, 32, 128], BF16)
    CoR = carve(38, [128, 32, 128], BF16)
    CoI = carve(46, [128, 32, 128], BF16)
    g1bc = carve(54, [128, 2, 1024])
    g2bc = carve(62, [128, 2, 1024])

    TWO_PI = 2.0 * PI
    a_in = [carve(50 + 0.5 * i, [32, 128]) for i in range(4)]
    for t_, src in zip(a_in, (a_re, a_im, ss5r, ss5i)):
        dma("sp", t_.rearrange("g (d p) -> g d p", d=2), src.rearrange("d g p -> g d p"), G["ain"], (), ["a_in"])
    ldt = sb("ldt", [128, 32])
    for d_ in range(2):
        dma("sp", ldt[d_ * 64:(d_ + 1) * 64, :], rowbc(log_dt[d_:d_ + 1, :], 64), G["ldt"], (), ["ldt"])
    Bt = [carve(174, [128, 32, 16]), carve(176, [128, 32, 16])]
    Bbar = [carve(178, [128, 32, 16]), carve(180, [128, 32, 16])]
    Ct = [carve(182, [128, 32, 16]), carve(184, [128, 32, 16])]
    Cin = [carve(186, [128, 4, 128]), carve(188, [128, 4, 128])]
    dbc = carve(190, [128, 32, 16])
    for ri, src in enumerate((b_re, b_im)):
        for d_ in range(2):
            dma("sp", Bt[ri][d_ * 64:(d_ + 1) * 64], src.rearrange("g p c -> p g c"), G["bt"], (), ["Bt"], noncontig=True)
    for ri, src in enumerate((c_re, c_im)):
        for dup in range(2):
            dma("sp", Cin[ri][:, :, dup * 64:(dup + 1) * 64], src.rearrange("(blk gg) c p -> (gg c) blk p", blk=4),
                G["cin"], (), ["Cin"])
    dma("sp", dbc.rearrange("p g c -> p (g c)"), rowbc(s5_d, 128), G["dbc"], (), ["dbc"])

    sm = {}
    for nm in ("ar", "ai", "s0r", "s0i", "dt", "lr", "li", "abr", "abi", "den", "nr", "ni", "cr", "ci", "t1", "t2", "t3",
               "p8r", "p8i", "n7r", "n7i", "p7r", "p7i", "e7"):
        sm[nm] = sb("sm_" + nm, [128, 32])
    for k_, nm in enumerate(("ar", "ai", "s0r", "s0i")):
        mm(psD[0][:, k_ * 32:(k_ + 1) * 32], a_in[k_], ident_f[0:32, 0:32], True, True, ["a_in", "ident_f"], ["pss5a"])
        cp("dve", sm[nm][:], psD[0][:, k_ * 32:(k_ + 1) * 32], ["pss5a"], ["sm_" + nm])
    for ri in range(2):
        for blk in range(4):
            mm(psD[1 + ri][:, blk * 128:(blk + 1) * 128], Cin[ri][:, blk, :], ident_f[:], True, True, ["Cin", "ident_f"], ["pss5c%d" % ri])
        cp("act", Ct[ri].rearrange("p g c -> p (g c)"), psD[1 + ri][:, 0:512], ["pss5c%d" % ri], ["Ct"])

    def K(*names):
        return ["sm_" + n for n in names]

    act(sm["dt"][:], ldt[:], AF.Exp, ["ldt"], K("dt"))
    tt("dve", sm["lr"][:], sm["ar"][:], sm["dt"][:], ALU.mult, K("ar", "dt"), K("lr"))
    tt("dve", sm["li"][:], sm["ai"][:], sm["dt"][:], ALU.mult, K("ai", "dt"), K("li"))

    condT = sb("condT", [128, 8, 2], F32)
    scT = sb("scT", [128, 8, 2], BF16)
    modT = sb("modT", [128, 4, 8, 2], F32)
    bada_f = carve(86, [1, 6144])

    for i in range(2):
        dma("sp", condT[:, :, i], cond[i].rearrange("(kc p) -> p kc", p=128), G["cond"], (), ["condT"], noncontig=True)
    dma("sp", bada_f, b_ada[:, :], G["bada"], (), ["bada_f"])
    act(scT[:], condT[:], AF.Silu, ["condT"], ["scT"])

    wada = [carve(70, [128, 8, 512], BF16), carve(78, [128, 8, 512], BF16)]
    d_wada = [S.dsem("wada0"), S.dsem("wada1")]
    w_ada_v = w_ada.rearrange("(kc p) n -> p kc n", p=128)
    pmod = psT[0][:].bitcast(F32)[:, 0:64]
    gbanks = [(pg_, ["psT1"]) for pg_ in [psT[1][:].bitcast(F32)]]
    gbanks += [(psD[i_][:, h_ * 512:(h_ + 1) * 512], ["psD%d_%d" % (i_, h_)]) for i_ in range(3) for h_ in range(2)]
    gcnt = [0]
    gpend = []
    pg = psT[1][:].bitcast(F32)
    order = [4, 5, 10, 11, 0, 1, 2, 3, 6, 7, 8, 9]
    for n_, pc in enumerate(order):
        v, hh = pc // 2, pc % 2
        bufi = n_ % 2
        wt = wada[bufi]
        key = "wada%d" % bufi
        c0 = v * 1024 + hh * 512
        dma("pool", wt, w_ada_v[:, :, c0:c0 + 512], d_wada[bufi], (), [key])
        if v in (0, 1, 3, 4):
            vi = {0: 0, 1: 1, 3: 2, 4: 3}[v]
            for f4 in range(4):
                fc = hh * 4 + f4
                o_ap = pmod[:, (vi * 8 + fc) * 2:(vi * 8 + fc) * 2 + 2]
                for kc in range(8):
                    mm(o_ap, wt[:, kc, f4 * 128:(f4 + 1) * 128], scT[:, kc, :], kc == 0, False, [key, "scT"], ["psT0"])
                mm(o_ap, bada_f[0:1, c0 + f4 * 128: c0 + (f4 + 1) * 128], ones_f[0:1, 0:2], False, True,
                   ["bada_f", "ones_f"], ["psT0"])
        else:
            gt = g1bc if v == 2 else g2bc
            for i in range(2):
                bank, bkeys = gbanks[gcnt[0] % len(gbanks)]
                gcnt[0] += 1
                for kc in range(8):
                    lhs = pat(scT[:, kc, i:i + 1], [[0, 128]])
                    mm(bank, lhs, wt[:, kc, :], kc == 0, False, [key, "scT"], bkeys)
                mm(bank, ones_f[0:1, 0:128], bada_f[0:1, c0:c0 + 512], False, True, ["bada_f", "ones_f"], bkeys)
                dst_ = gt[:, i, hh * 512:(hh + 1) * 512]
                if gcnt[0] == 1:
                    cp("act", dst_, bank, bkeys, ["gbc"])
                else:
                    gpend.append((dst_, bank, bkeys))

    def finish_g():
        for dst_, bank, bkeys in gpend:
            cp("act", dst_, bank, bkeys, ["gbc"])

    def finish_mod():
        cp("dve", modT[:].rearrange("p v f i -> p (v f i)"), pmod, ["psT0"], ["modT"])
        for vi in (1, 3):
            ts("dve", modT[:, vi], modT[:, vi], 1.0, None, ALU.add, None, ["modT"], ["modT"])

    if "mod" in dbg:
        dbg_out["modT"] = (modT[:].rearrange("p v f i -> p (v f i)"), [128, 64], ["modT"])
        dbg_out["g1bc"] = (g1bc[0:1].rearrange("p a n -> p (a n)"), [1, 2048], ["gbc"])

    if "stop_mod" in dbg:
        return finish()
    rd = sb("rd", [128, 8])
    lg = sb("lg", [128, 8])
    kcol = sb("kcol", [128, 8])
    dcol = sb("dcol", [128, 8])
    tq = [carve(46 + 0.5 * i, [128, 128]) for i in range(6)]
    dpos, dneg, gem, lem, ip1, rmi = tq
    jrev = sb("jrev", [128, 1])
    ftmp = carve(49, [128, 256])
    tmpa = carve(49, [128, 128])
    tmpb = carve(49.5, [128, 128])
    SC = 128.0 ** -0.5
    dma("sp", rd[:], rowbc(ret_decay, 128), G["rd"], (), ["rd"])
    act(lg[:], rd[:], AF.Exp, ["rd"], ["lg"], scale=-1.0)
    act(lg[:], lg[:], AF.Ln, ["lg"], ["lg"], bias=ones_f[:, 0:1])
    ts("dve", lg[:], lg[:], -1.0, None, ALU.mult, None, ["lg"], ["lg"])
    ts("dve", dpos, dmat[:], 0.0, None, ALU.max, None, ["dmat"], ["tq"])
    ts("dve", dneg, dmat[:], -1.0, 0.0, ALU.mult, ALU.max, ["dmat"], ["tq"])
    ts("dve", gem, dmat[:], 0.0, None, ALU.is_ge, None, ["dmat"], ["tq"])
    ts("dve", lem, dmat[:], 0.0, None, ALU.is_le, None, ["dmat"], ["tq"])
    ts("dve", ip1, ii[:], 1.0, None, ALU.add, None, ["ii"], ["tq"])
    ts("dve", rmi, ii[:], -1.0, 128.0, ALU.mult, ALU.add, ["ii"], ["tq"])
    ts("dve", jrev[:], jj[:], -1.0, 127.0, ALU.mult, ALU.add, ["jj"], ["jrev"])
    for h in range(4):
        act(tmpa, dpos, AF.Exp, ["tq", "lg"], ["tmpa"], scale=lg[:, h:h + 1])
        tt("dve", tmpa, tmpa, gem, ALU.mult, ["tmpa", "tq"], ["tmpa"])
        act(tmpb, dneg, AF.Exp, ["tq", "lg"], ["tmpb"], scale=lg[:, 4 + h:5 + h])
        stt("dve", tmpb, tmpb, SC, lem, ALU.mult, ALU.mult, ["tmpb", "tq"], ["tmpb"])
        stt("dve", maskT[:, h, :], tmpa, SC, tmpb, ALU.mult, ALU.add, ["tmpa", "tmpb"], ["rtab"])
        act(colf[:, h, :], ip1, AF.Exp, ["tq", "lg"], ["rtab"], scale=lg[:, h:h + 1])
        act(colb[:, h, :], rmi, AF.Exp, ["tq", "lg"], ["rtab"], scale=lg[:, 4 + h:5 + h])
        act(kcol[:, h:h + 1], jrev[:], AF.Exp, ["jrev", "lg"], ["kcol"], scale=lg[:, h:h + 1])
        act(kcol[:, 4 + h:5 + h], jj[:], AF.Exp, ["jj", "lg"], ["kcol"], scale=lg[:, 4 + h:5 + h])
    act(dcol[:], lg[:], AF.Exp, ["lg"], ["dcol"], scale=128.0)
    ts("dve", kcol[:], kcol[:], SC, None, ALU.mult, None, ["kcol"], ["kcol"])
    cp("dve", kdf_tab, pat(kcol[:, 0:4], [[1, 4], [0, 128]]), ["kcol"], ["rtab"])
    cp("dve", kdb_tab, pat(kcol[:, 4:8], [[1, 4], [0, 128]]), ["kcol"], ["rtab"])
    cp("dve", dec_tab.rearrange("p d h e -> p (d h) e"), pat(dcol[:, 0:8], [[1, 8], [0, 128]]), ["dcol"], ["rtab"])
    if "stop_ret" in dbg:
        dbg_out["lg"] = (lg[:], [128, 8], ["lg"], F32)
        dbg_out["kcol"] = (kcol[:], [128, 8], ["kcol"], F32)
        dbg_out["colf"] = (colf.rearrange("p h i -> p (h i)"), [128, 512], ["rtab"], F32)
        dbg_out["maskT"] = (maskT.rearrange("p h i -> p (h i)"), [128, 512], ["rtab"], F32)
        return finish()
    def cplx_pow(out_r, out_i, er, ei, shape_n, keys_r, keys_w, tmp, ft_=None):
        ftmp_l = ftmp if ft_ is None else ft_
        mag, c_, s_ = tmp
        act(mag, er, AF.Exp, keys_r, keys_w)
        def flat(a):
            return a if len(a.shape) == 2 else a.rearrange("p a b -> p (a b)")
        n_ = int(np.prod(er.shape[1:]))
        for dstt, shift in ((s_, 64.0), (c_, 64.25)):
            d2 = flat(dstt)
            ts("dve", d2, flat(ei), 1.0 / TWO_PI, shift, ALU.mult, ALU.add, keys_r, keys_w)
            cp("dve", itmp[:, 0:n_], d2, keys_w, keys_w + ["itmp"])
            cp("dve", ftmp_l[:, 0:n_], itmp[:, 0:n_], ["itmp"], ["tmpa", "tmpb"])
            tt("dve", d2, d2, ftmp_l[:, 0:n_], ALU.subtract, keys_w + ["tmpa", "tmpb"], keys_w)
            ts("dve", ftmp_l[:, 0:n_], d2, 0.5, None, ALU.is_ge, None, keys_w, ["tmpa", "tmpb"])
            tt("dve", d2, d2, ftmp_l[:, 0:n_], ALU.subtract, keys_w + ["tmpa", "tmpb"], keys_w)
            act(d2, d2, AF.Sin, keys_w, keys_w, scale=TWO_PI)
        tt("dve", out_r, mag, c_, ALU.mult, keys_w, keys_w)
        tt("dve", out_i, mag, s_, ALU.mult, keys_w, keys_w)

    if "stop_s5a" in dbg:
        dbg_out["lr"] = (sm["lr"][:], [128, 32], ["sm_lr"], F32)
        dbg_out["Ct0"] = (Ct[0].rearrange("p g c -> p (g c)"), [128, 512], ["Ct"], F32)
        return finish()
    cplx_pow(sm["abr"][:], sm["abi"][:], sm["lr"][:], sm["li"][:], None, K("lr", "li"), K("abr", "abi", "t1", "t2", "t3"),
             (sm["t1"][:], sm["t2"][:], sm["t3"][:]))
    if "stop_s5b" in dbg:
        for nm_ in ("ar", "ai", "dt", "lr", "li", "abr", "abi", "t1", "t2", "t3"):
            dbg_out[nm_] = (sm[nm_][:], [128, 32], ["sm_" + nm_], F32)
        return finish()
    tt("dve", sm["den"][:], sm["ar"][:], sm["ar"][:], ALU.mult, K("ar"), K("den"))
    tt("dve", sm["t1"][:], sm["ai"][:], sm["ai"][:], ALU.mult, K("ai"), K("t1"))
    tt("dve", sm["den"][:], sm["den"][:], sm["t1"][:], ALU.add, K("den", "t1"), K("den"))
    S.op("dve", lambda e: e.reciprocal(out=sm["den"][:], in_=sm["den"][:]), K("den"), K("den"))
    ts("dve", sm["t2"][:], sm["abr"][:], -1.0, None, ALU.add, None, K("abr"), K("t2"))
    tt("dve", sm["nr"][:], sm["t2"][:], sm["ar"][:], ALU.mult, K("t2", "ar"), K("nr"))
    tt("dve", sm["t1"][:], sm["abi"][:], sm["ai"][:], ALU.mult, K("abi", "ai"), K("t1"))
    tt("dve", sm["nr"][:], sm["nr"][:], sm["t1"][:], ALU.add, K("nr", "t1"), K("nr"))
    tt("dve", sm["ni"][:], sm["abi"][:], sm["ar"][:], ALU.mult, K("abi", "ar"), K("ni"))
    tt("dve", sm["t1"][:], sm["t2"][:], sm["ai"][:], ALU.mult, K("t2", "ai"), K("t1"))
    tt("dve", sm["ni"][:], sm["ni"][:], sm["t1"][:], ALU.subtract, K("ni", "t1"), K("ni"))
    tt("dve", sm["cr"][:], sm["nr"][:], sm["den"][:], ALU.mult, K("nr", "den"), K("cr"))
    tt("dve", sm["ci"][:], sm["ni"][:], sm["den"][:], ALU.mult, K("ni", "den"), K("ci"))
    crb = pat(sm["cr"][:], [[1, 32], [0, 16]])
    cib = pat(sm["ci"][:], [[1, 32], [0, 16]])
    tb0 = carve(192, [128, 32, 16])
    tt("dve", Bbar[0], Bt[0], crb, ALU.mult, ["Bt"] + K("cr"), ["Bbar"])
    tt("dve", tb0, Bt[1], cib, ALU.mult, ["Bt"] + K("ci"), ["tb0"])
    tt("dve", Bbar[0], Bbar[0], tb0, ALU.subtract, ["Bbar", "tb0"], ["Bbar"])
    tt("dve", Bbar[1], Bt[1], crb, ALU.mult, ["Bt"] + K("cr"), ["Bbar"])
    tt("dve", tb0, Bt[0], cib, ALU.mult, ["Bt", "Bbar"] + K("ci"), ["tb0"])
    tt("dve", Bbar[1], Bbar[1], tb0, ALU.add, ["Bbar", "tb0"], ["Bbar"])

    EL = sb("EL", [128, 8])
    ER = sb("ER", [128, 8])
    ts("dve", EL[0:64, :], ii[0:64, 0:8], -1.0, None, ALU.mult, None, ["ii"], ["EL"])
    cp("dve", EL[64:128, :], ii[64:128, 0:8], ["ii"], ["EL"])
    ts("dve", ER[:], EL[:], -1.0, None, ALU.mult, None, ["EL"], ["ER"])
    E8 = sb("E8", [128, 8])
    ts("dve", E8[:], ii[:, 0:8], 1.0, 8.0, ALU.add, ALU.mult, ["ii"], ["E8"])
    PW = {}
    ptmp = [carve(18, [128, 8, 32]), carve(19, [128, 8, 32]), carve(20, [128, 8, 32]), carve(21, [128, 8, 32])]
    for nm, E_, off in (("L", EL, 14), ("R", ER, 16)):
        pr = carve(off, [128, 8, 32])
        pi_ = carve(off + 1, [128, 8, 32])
        Eb = pat(E_[:], [[1, 8], [0, 32]])
        tt("dve", ptmp[0], Eb, pat(sm["lr"][:], [[0, 8], [1, 32]]), ALU.mult, [nm == "L" and "EL" or "ER"] + K("lr"), ["ptmp"])
        tt("dve", ptmp[1], Eb, pat(sm["li"][:], [[0, 8], [1, 32]]), ALU.mult, [nm == "L" and "EL" or "ER"] + K("li"), ["ptmp"])
        cplx_pow(pr, pi_, ptmp[0], ptmp[1], None, ["ptmp"], ["ptmp", "P" + nm], (ptmp[2], ptmp[3], ptmp[0]))
        PW[nm] = (pr, pi_)

    def single_pow(nr_, ni_, e_lo, e_hi):
        for lo, hi, ev in ((0, 64, e_lo), (64, 128, e_hi)):
            ts("dve", sm["t1"][lo:hi], sm["lr"][lo:hi], float(ev), None, ALU.mult, None, K("lr"), K("t1"))
            ts("dve", sm["t2"][lo:hi], sm["li"][lo:hi], float(ev), None, ALU.mult, None, K("li"), K("t2"))
        cplx_pow(sm[nr_][:], sm[ni_][:], sm["t1"][:], sm["t2"][:], None, K("t1", "t2"), K(nr_, ni_, "t3", "den", "nr"),
                 (sm["t3"][:], sm["den"][:], sm["nr"][:]))

    single_pow("p8r", "p8i", 8, 8)
    single_pow("n7r", "n7i", -7, 0)
    single_pow("p7r", "p7i", 7, 0)

    Lr = carve(110, [128, 32, 8, 16])
    Li = carve(126, [128, 32, 8, 16])
    Rr = carve(142, [128, 32, 8, 16])
    Rn = carve(158, [128, 32, 8, 16])
    tbig = carve(70, [128, 32, 8, 16])

    def bc_x(x):
        return pat(x, [[16, 32], [0, 8], [1, 16]])

    def bc_p(p_):
        return pat(p_, [[1, 32], [32, 8], [0, 16]])

    WKEY = ["wada0", "wada1"]
    for (Xr, Xi, (Pr, Pi_), Or, Oi, neg, okey) in ((Bbar[0], Bbar[1], PW["L"], Lr, Li, False, "Lset"),
                                                  (Ct[0], Ct[1], PW["R"], Rr, Rn, True, "Rset")):
        xk = ["Bbar"] if okey == "Lset" else ["Ct"]
        pk = ["PL"] if okey == "Lset" else ["PR"]
        tt("dve", Or, bc_x(Xr), bc_p(Pr), ALU.mult, xk + pk, [okey + "r"])
        tt("dve", tbig, bc_x(Xi), bc_p(Pi_), ALU.mult, xk + pk, WKEY)
        tt("dve", Or, Or, tbig, ALU.subtract, [okey + "r"] + WKEY, [okey + "r"])
        tt("dve", Oi, bc_x(Xr), bc_p(Pi_), ALU.mult, xk + pk, [okey + "i"])
        tt("dve", tbig, bc_x(Xi), bc_p(Pr), ALU.mult, xk + pk + WKEY, WKEY)
        if neg:
            stt("dve", Oi, Oi, -1.0, tbig, ALU.mult, ALU.subtract, [okey + "i"] + WKEY, [okey + "i"])
        else:
            tt("dve", Oi, Oi, tbig, ALU.add, [okey + "i"] + WKEY, [okey + "i"])

    p8rb = pat(sm["p8r"][:], [[1, 32], [0, 128]])
    p8ib = pat(sm["p8i"][:], [[1, 32], [0, 128]])
    Rr3 = Rr.rearrange("p g t c -> p g (t c)")
    Rn3 = Rn.rearrange("p g t c -> p g (t c)")
    tb3 = tbig.rearrange("p g t c -> p g (t c)")
    tb4 = carve(86, [128, 32, 128])
    tt("dve", tb3, Rr3, p8rb, ALU.mult, ["Rsetr"] + K("p8r") + WKEY, WKEY)
    tt("dve", tb4, Rn3, p8ib, ALU.mult, ["Rseti", "bada_f"] + K("p8i"), ["bada_f"])
    tt("dve", CoR, tb3, tb4, ALU.add, WKEY + ["bada_f"], ["CoR"])
    tt("dve", tb3, Rn3, p8rb, ALU.mult, ["Rseti"] + K("p8r") + WKEY, WKEY)
    tt("dve", tb4, Rr3, p8ib, ALU.mult, ["Rsetr", "bada_f"] + K("p8i"), ["bada_f"])
    tt("dve", CoI, tb3, tb4, ALU.subtract, WKEY + ["bada_f"], ["CoI", "tq", "tmpa", "tmpb", "a_in"])

    finish_mod()
    finish_g()
    Lb = [carve(94, [128, 32, 128], BF16), carve(102, [128, 32, 128], BF16)]
    cp("dve", Lb[0], Lr.rearrange("p g m c -> p g (m c)"), ["Lsetr"], ["Lb", "bada_f"])
    cp("act", Lb[1], Li.rearrange("p g m c -> p g (m c)"), ["Lseti"], ["Lb", "bada_f"])
    Lr3 = Lr.rearrange("p g m c -> p g (m c)")
    Li3 = Li.rearrange("p g m c -> p g (m c)")
    for ri, Wd in enumerate((WsR, WsI)):
        for g8 in range(4):
            pb = psT[(ri * 4 + g8) % 2]
            pkk = "psT%d" % ((ri * 4 + g8) % 2)
            for k_ in range(8):
                tr(pb[:, k_ * 128:(k_ + 1) * 128], Lb[ri][:, g8 * 8 + k_, :], ident_b[:], ["Lb", "ident_b"], [pkk])
            cp("act" if g8 % 2 else "dve", Wd[:, g8 * 8:(g8 + 1) * 8, :].rearrange("p g n -> p (g n)"), pb[:], [pkk], ["Ws"])
    pm = sb("pm", [128, 1])
    ft = sb("ft", [128, 8])
    mF = sb("mF", [128, 8])
    mB = sb("mB", [128, 8])
    S.op("pool", lambda e: e.iota(iota_i[:, 0:1], pattern=[[0, 1]], base=0, channel_multiplier=1), ["iota_i"], ["iota_i"])
    S.op("dve", lambda e: e.tensor_single_scalar(out=iota_i[:, 1:2], in_=iota_i[:, 0:1], scalar=4, op=ALU.arith_shift_right),
         ["iota_i"], ["iota_i"])
    cp("dve", pm[:], iota_i[:, 1:2], ["iota_i"], ["pm"])
    ts("dve", ft[:], ii[:, 0:8], pm[:, 0:1], None, ALU.subtract, None, ["ii", "pm"], ["ft"])
    ts("dve", mF[:], ft[:], 0.0, None, ALU.is_ge, None, ["ft"], ["mF"])
    ts("dve", mB[:], ft[:], 0.0, None, ALU.is_le, None, ["ft"], ["mB"])
    mFb = pat(mF[:], [[1, 8], [0, 16]])
    mBb = pat(mB[:], [[1, 8], [0, 16]])
    tmA = carve(192, [128, 8, 16])
    tmB = carve(192.5, [128, 8, 16])
    idv = ident_f[:].rearrange("p (t c) -> p t c", t=8)
    mlo = sb("mlo", [128, 2])
    ts("dve", mlo[:, 0:1], jj[:], 64.0, None, ALU.is_lt, None, ["jj"], ["mlo"])
    ts("dve", mlo[:, 1:2], jj[:], 64.0, None, ALU.is_ge, None, ["jj"], ["mlo"])
    mlo_b = pat(mlo[:, 0:1], [[0, 128]])
    mhi_b = pat(mlo[:, 1:2], [[0, 128]])
    mlo4 = pat(mlo[:, 0:1], [[0, 4], [0, 128]])
    mhi4 = pat(mlo[:, 1:2], [[0, 4], [0, 128]])
    mF4 = pat(mF[:, 0:1], [[0, 4], [1, 8], [0, 16]])
    mB4 = pat(mB[:, 0:1], [[0, 4], [1, 8], [0, 16]])
    id4 = pat(ident_f[:, 0:1], [[0, 4], [16, 8], [1, 16]])
    for g4 in range(8):
        par = g4 % 2
        gs = slice(g4 * 4, g4 * 4 + 4)
        rm = [carve(70 + 4 * par + k_, [128, 4, 128], BF16) for k_ in range(4)]
        rk = "rmask%d" % par
        tt("dve", rm[0], Rr3[:, gs, :], mlo4, ALU.mult, ["Rsetr", "mlo"], [rk] + (WKEY if g4 < 2 else []))
        tt("pool", rm[1], Rn3[:, gs, :], mlo4, ALU.mult, ["Rseti", "mlo"], [rk])
        tt("dve", rm[2], Rr3[:, gs, :], mhi4, ALU.mult, ["Rsetr", "mlo"], [rk])
        tt("pool", rm[3], Rn3[:, gs, :], mhi4, ALU.mult, ["Rseti", "mlo"], [rk])
        pb = psD[1 + par][:]
        pk = "psfull%d" % (1 + par)
        pbv = pb.rearrange("p (g f n) -> p g f n", g=4, f=2)
        for k_ in range(4):
            g_ = g4 * 4 + k_
            mm(pbv[:, k_, 0, :], Lb[0][:, g_, :], rm[0][:, k_, :], True, False, ["Lb", rk], [pk])
            mm(pbv[:, k_, 0, :], Lb[1][:, g_, :], rm[1][:, k_, :], False, True, ["Lb", rk], [pk])
            mm(pbv[:, k_, 1, :], Lb[0][:, g_, :], rm[2][:, k_, :], True, False, ["Lb", rk], [pk])
            mm(pbv[:, k_, 1, :], Lb[1][:, g_, :], rm[3][:, k_, :], False, True, ["Lb", rk], [pk])
        tA = carve(86 + 4 * par, [128, 4, 8, 16])
        tB = carve(88 + 4 * par, [128, 4, 8, 16])
        ka, kb = "tmA%d" % par, "tmB%d" % par
        tt("dve", tA, pbv[:, :, 0, :].rearrange("p g (t c) -> p g t c", t=8), mF4, ALU.mult, [pk, "mF"], [ka, "bada_f"])
        tt("dve", tB, pbv[:, :, 1, :].rearrange("p g (t c) -> p g t c", t=8), mB4, ALU.mult, [pk, "mB"], [kb, "bada_f"])
        tt("dve", tA, tA, tB, ALU.add, [ka, kb], [ka])
        tt("dve", tB, id4, pat(dbc[:, g4 * 4, 0:1], [[16, 4], [0, 8], [1, 16]]), ALU.mult, ["ident_f", "dbc", kb], [kb])
        tt("dve", Mtoe[:, gs, :].rearrange("p g (t c) -> p g t c", t=8), tA, tB, ALU.add, [ka, kb], ["Mtoe", "PL", "PR", "ptmp"])

    AA = sb("AA", [128, 2, 32])
    AIs = sb("AIs", [128, 2, 32])
    cp("dve", AA[:], pat(sm["p8r"][:], [[0, 2], [1, 32]]), K("p8r"), ["AA"])
    ts("dve", AIs[:, 0, :], sm["p8i"][:], -1.0, None, ALU.mult, None, K("p8i"), ["AIs"])
    cp("dve", AIs[:, 1, :], sm["p8i"][:], K("p8i"), ["AIs"])
    Z0 = sb("Z0", [128, 2, 32])
    tt("dve", Z0[:, 0, :], sm["s0r"][:], sm["n7r"][:], ALU.mult, K("s0r", "n7r"), ["Z0"])
    tt("dve", sm["t1"][:], sm["s0i"][:], sm["n7i"][:], ALU.mult, K("s0i", "n7i"), K("t1"))
    tt("dve", Z0[:, 0, :], Z0[:, 0, :], sm["t1"][:], ALU.subtract, ["Z0"] + K("t1"), ["Z0"])
    tt("dve", Z0[:, 1, :], sm["s0r"][:], sm["n7i"][:], ALU.mult, K("s0r", "n7i"), ["Z0"])
    tt("dve", sm["t1"][:], sm["s0i"][:], sm["n7r"][:], ALU.mult, K("s0i", "n7r"), K("t1"))
    tt("dve", Z0[:, 1, :], Z0[:, 1, :], sm["t1"][:], ALU.add, ["Z0"] + K("t1"), ["Z0"])

    if "s5mat" in dbg:
        dbg_out["Mtoe"] = (Mtoe[:, 0:4, :].rearrange("p g n -> p (g n)"), [128, 512], ["Mtoe"], BF16)
        dbg_out["WsR"] = (WsR[:, 0:4, :].rearrange("p g n -> p (g n)"), [128, 512], ["Ws"], BF16)
        dbg_out["CoR"] = (CoR[:, 0:4, :].rearrange("p g n -> p (g n)"), [128, 512], ["CoR"], BF16)
        dbg_out["CoI"] = (CoI[:, 0:4, :].rearrange("p g n -> p (g n)"), [128, 512], ["CoI"], BF16)
        dbg_out["Z0"] = (Z0[:].rearrange("p a g -> p (a g)"), [128, 64], ["Z0"], F32)
        dbg_out["maskT"] = (maskT.rearrange("p h i -> p (h i)"), [128, 512], ["rtab"], F32)

    barrier()
    def main_phases():
        w_in_bf = carve(70, [128, 8, 2560], BF16)
        w_glu_bf = carve(110, [128, 4, 512], BF16)
        mixT = carve(114, [128, 8, 1536], BF16)
        d_win = S.dsem("win", batch=True)
        d_win2 = S.dsem("win2", batch=True)
        w_in_v = w_in.rearrange("(kc p) n -> p kc n", p=128)
        for k2 in range(4):
            dma("pool", w_in_bf[:, 2 * k2:2 * k2 + 2, :], w_in_v[:, 2 * k2:2 * k2 + 2, :], d_win, BARK, ["w_in"])
        dma("pool", w_glu_bf, w_glu.rearrange("(kc p) n -> p kc n", p=128), d_win, BARK, ["w_glu"])
        bgluT = sb("bgluT", [128, 4])
        dma("sp", bgluT[:], b_glu.rearrange("(c p) -> p c", p=128), d_ld1, BARK, ["bgluT"], noncontig=True)

        st6 = sb("st6", [128, 4, 6])
        mv = sb("mv", [128, 4, 2])
        rstd = sb("rstd", [128, 4])
        nmr = sb("nmr", [128, 4])
        Sst = [sb("Sst%d" % d_, [128, 4, 128]) for d_ in range(2)]
        Hmid = sb("Hmid", [128, 2, 32])
        zero64 = sb("zero64", [128, 2, 2, 32])
        memset("dve", zero64[:], 0.0, ["zero64"])
        cp("dve", Hmid[:], Z0[:], ["Z0"], ["Hmid"])
        d_x = [S.dsem("x0"), S.dsem("x1")]
        d_st = S.dsem("st", batch=True)
        rr = [0]

        def halfbank():
            rr[0] = (rr[0] + 1) % 6
            i = rr[0]
            return psD[i // 2][:, (i % 2) * 512:(i % 2) * 512 + 512], "psD%d_%d" % (i // 2, i % 2)

        def ln_stats(src, nchunks, csz, skeys, col=0, tiles=None, tag=None):
            if tiles is None:
                st6_, mv_, rs_, nm_ = st6, mv, rstd, nmr
                sk, mk, rk_, nk_ = "st6_%d" % col, "mv_%d" % col, "rstd_%d" % col, "nmr_%d" % col
                r0 = 2 * col
            else:
                st6_, mv_, rs_, nm_ = tiles
                sk = mk = rk_ = nk_ = tag
                r0 = 0
            for k_ in range(nchunks):
                S.op("dve", lambda e, k_=k_: e.bn_stats(out=st6_[:, r0 + k_, :], in_=src[:, k_ * csz:(k_ + 1) * csz]), skeys, [sk])
            S.op("dve", lambda e: e.bn_aggr(out=mv_[:, col, :], in_=st6_[:, r0:r0 + nchunks, :]), [sk], [mk])
            act(rs_[:, col:col + 1], mv_[:, col, 1:2], AF.Ln, [mk], [rk_], bias=epsc[:, 0:1])
            act(rs_[:, col:col + 1], rs_[:, col:col + 1], AF.Exp, [rk_], [rk_], scale=-0.5)
            stt("dve", nm_[:, col:col + 1], mv_[:, col, 0:1], -1.0, rs_[:, col:col + 1], ALU.mult, ALU.mult, [mk, rk_], [nk_])

        def stage1a(xsrc, ntok, ci, hT, vsh, vsc):
            XIN = [carve(162, [128, 1024]), carve(166, [128, 1024])]
            XN = [carve(170, [128, 1024], BF16), carve(172, [128, 1024], BF16)]
            EV32 = [carve(174, [128, 8, 128]), carve(178, [128, 8, 128])]
            for t in range(ntok // 128):
                b_ = t % 2
                dma("sp", XIN[b_], xsrc[t * 128:(t + 1) * 128, :], d_x[b_], BARK, ["xin%d" % b_])
                ln_stats(XIN[b_], 2, 512, ["xin%d" % b_], col=b_)
                act(XN[b_], XIN[b_], AF.Identity, ["xin%d" % b_, "rstd_%d" % b_, "nmr_%d" % b_], ["xn%d" % b_], bias=nmr[:, b_:b_ + 1],
                    scale=rstd[:, b_:b_ + 1])
                for c in range(8):
                    tr(psT[b_][:, c * 128:(c + 1) * 128], XN[b_][:, c * 128:(c + 1) * 128], ident_b[:], ["xn%d" % b_], ["psT%d" % b_])
                tmp32 = EV32[b_]
                sc_bc = pat(modT[:, vsc, 0, ci:ci + 1], [[2, 8], [0, 128]])
                sh_bc = pat(modT[:, vsh, 0, ci:ci + 1], [[2, 8], [0, 128]])
                tt("dve", tmp32, psT[b_][:].rearrange("p (c n) -> p c n", c=8), sc_bc, ALU.mult, ["psT%d" % b_, "modT"], ["ev32_%d" % b_])
                tt("pool", hT[:, :, t * 128:(t + 1) * 128], tmp32, sh_bc, ALU.add, ["ev32_%d" % b_, "modT"], ["hT"])

        def proj_fm(hT, tg, col0, dst, toggle):
            bank, bk = halfbank()
            for c in range(8):
                mm(bank, w_in_bf[:, c, col0:col0 + 128], hT[:, c, tg * 512:(tg + 1) * 512], c == 0, c == 7, ["w_in", "hT"], [bk])
            cp("act" if toggle else "dve", dst, bank, [bk], ["proj"])

        def proj_tm(hT, t, col0):
            bank, bk = halfbank()
            for c in range(8):
                mm(bank, hT[:, c, t * 128:(t + 1) * 128], w_in_bf[:, c, col0:col0 + 512], c == 0, c == 7, ["w_in", "hT"], [bk])
            return bank, bk

        def proj_u(hT, ntok, u_cm):
            nch = ntok // 8
            for m in range(8):
                bank, bk = halfbank()
                for c in range(8):
                    lhs = pat(hT[:, c, m:m + 1], [[8, nch]])
                    mm(bank[0:nch, :], lhs, w_in_bf[:, c, 2048:2560], c == 0, c == 7, ["w_in", "hT"], [bk])
                cp("act" if m % 2 else "dve", u_cm[0:nch, :, m, :], bank[0:nch, :].rearrange("p (g c) -> p g c", g=32), [bk], ["u_cm"])

        def kv_update(d_, kd, vt, first_keys):
            bank, bk = halfbank()
            for h in range(4):
                mm(bank[:, h * 128:(h + 1) * 128], kd[:, h * 128:(h + 1) * 128], vt[:, h * 128:(h + 1) * 128], True, True,
                   first_keys, [bk])
            tt("pool", Sst[d_][:], Sst[d_][:], dec_tab[:, d_], ALU.mult, ["Sst%d" % d_, "rtab"], ["Sst%d" % d_])
            tt("dve", Sst[d_][:].rearrange("p h e -> p (h e)"), Sst[d_][:].rearrange("p h e -> p (h e)"), bank, ALU.add,
               ["Sst%d" % d_, bk], ["Sst%d" % d_])

        def s5_sums(u_cm, nch, nseq, H, u_arr):
            nk = nch // nseq
            for g8 in range(4):
                b_ = g8 % 2
                for gi in range(8):
                    g_ = g8 * 8 + gi
                    src = u_cm[0:nch, g_].rearrange("p m c -> p (m c)")
                    tr(psT[b_][:, gi * 128:gi * 128 + nch], src, ident_b[0:nch, 0:nch], ["u_cm"], ["psT%d" % b_])
                cp("act" if g8 % 2 else "dve", u_arr[:, g8 * 8:(g8 + 1) * 8, :],
                   psT[b_][:].rearrange("p (g j) -> p g j", g=8)[:, :, 0:nch], ["psT%d" % b_], ["u_arr"])
            for g8 in range(4):
                for ri, Ws in enumerate((WsR, WsI)):
                    pk = "pss%d" % ri
                    for gi in range(8):
                        g_ = g8 * 8 + gi
                        mm(psD[ri][:, gi * nch:(gi + 1) * nch], Ws[:, g_, :], u_arr[:, g_, :], True, True, ["Ws", "u_arr"], [pk])
                    for sq in range(nseq):
                        pin = psD[ri][:, 0:8 * nch].rearrange("p (g s k) -> p g s k", g=8, s=nseq)
                        o_f = pat(H[0:64, sq, 0, ri, g8 * 8:g8 * 8 + 1], [[1, 8], [64, nk]])
                        cp("dve", o_f, pin[0:64, :, sq, :], [pk], ["H"])
                        o_b = pat(H[64:128, sq, 0, ri, g8 * 8:g8 * 8 + 1], [[1, 8], [64, nk]])
                        cp("act", o_b, rev_ap(pin[64:128, :, sq, :], 2), [pk], ["H"])

        def s5_scan(H, nseq, nk, init, lo=0):
            t1 = sb_scan[0][lo:128, 0:nseq]
            t2 = sb_scan[1][lo:128, 0:nseq]
            aab = pat(AA[lo:128, 0, 0:1], [[0, nseq], [1, 64]]).rearrange("p s (r g) -> p s r g", r=2)
            aib = pat(AIs[lo:128, 0, 0:1], [[0, nseq], [1, 64]]).rearrange("p s (r g) -> p s r g", r=2)
            for k_ in range(0 if "noscan" in dbg else (2 if "scan2" in dbg else nk)):
                prev = init[lo:128, 0:nseq] if k_ == 0 else H[lo:128, :, k_ - 1]
                tt("dve", t1, prev, aab, ALU.mult, ["H", "init"], ["sc1"])
                tt("dve", t2, rev_ap(prev, 2), aib, ALU.mult, ["H", "init"], ["sc2"])
                tt("dve", t1, t1, t2, ALU.add, ["sc1", "sc2"], ["sc1"])
                tt("dve", H[lo:128, :, k_], H[lo:128, :, k_], t1, ALU.add, ["H", "sc1"], ["H"])

        def s5_scan_blk(H, nk, init, lo=0, fill=True):
            P_ = slice(lo, 128)
            nb = nk // 8
            nv = nb - 1
            t1b = carve(138, [128, 16, 2, 32])
            t2b = carve(142, [128, 16, 2, 32])
            PAr = carve(146, [128, 8, 32])
            PAi = carve(147, [128, 8, 32])
            AIp = carve(148, [128, 8, 2, 32])
            er = carve(150, [128, 8, 32])
            ei = carve(151, [128, 8, 32])
            mg = carve(152, [128, 8, 32])
            cc = carve(153, [128, 8, 32])
            ft2 = carve(138, [128, 256])
            KB = ["blk_p"]
            E8b = pat(E8[:], [[1, 8], [0, 32]])
            tt("dve", er, E8b, pat(sm["lr"][:], [[0, 8], [1, 32]]), ALU.mult, ["E8", "sm_lr"], KB)
            tt("dve", ei, E8b, pat(sm["li"][:], [[0, 8], [1, 32]]), ALU.mult, ["E8", "sm_li"], KB)
            cplx_pow(PAr, PAi, er, ei, None, KB, KB, (mg, cc, er), ft_=ft2)
            ts("dve", AIp[:, :, 0, :], PAi, -1.0, None, ALU.mult, None, KB, KB)
            cp("dve", AIp[:, :, 1, :], PAi, KB, KB)
            KH = ["H", "init"] + KB

            def blkv(i, b0, n):
                return pat(H[P_, 0, b0 * 8 + i, 0, 0:1], [[512, n], [32, 2], [1, 32]])

            aa1 = pat(AA[P_, 0, 0:1], [[32, 2], [1, 32]])
            ai1 = pat(AIs[P_, 0, 0:1], [[32, 2], [1, 32]])
            ts1, ts2 = t1b[P_, 0], t2b[P_, 0]
            for k_ in range(8):
                prev = init[P_, 0] if k_ == 0 else H[P_, 0, k_ - 1]
                tt("dve", ts1, prev, aa1, ALU.mult, KH, ["sc1"])
                tt("dve", ts2, rev_ap(prev, 1), ai1, ALU.mult, KH, ["sc2"])
                tt("dve", ts1, ts1, ts2, ALU.add, ["sc1", "sc2"], ["sc1"])
                tt("dve", H[P_, 0, k_], H[P_, 0, k_], ts1, ALU.add, ["H", "sc1"], ["H"])
            aaN = pat(AA[P_, 0, 0:1], [[0, nv], [32, 2], [1, 32]])
            aiN = pat(AIs[P_, 0, 0:1], [[0, nv], [32, 2], [1, 32]])
            t1v, t2v = t1b[P_, 0:nv], t2b[P_, 0:nv]
            for i in range(1, 8):
                vp, vi = blkv(i - 1, 1, nv), blkv(i, 1, nv)
                tt("dve", t1v, vp, aaN, ALU.mult, KH, ["sc1"])
                tt("dve", t2v, rev_ap(vp, 2), aiN, ALU.mult, KH, ["sc2"])
                tt("dve", t1v, t1v, t2v, ALU.add, ["sc1", "sc2"], ["sc1"])
                tt("dve", vi, vi, t1v, ALU.add, ["H", "sc1"], ["H"])
            a8r = pat(PAr[P_, 7, 0:1], [[0, 2], [1, 32]])
            a8i = AIp[P_, 7]
            for b in range(1, nb):
                prev = H[P_, 0, (b - 1) * 8 + 7]
                cur = H[P_, 0, b * 8 + 7]
                tt("dve", ts1, prev, a8r, ALU.mult, KH, ["sc1"])
                tt("dve", ts2, rev_ap(prev, 1), a8i, ALU.mult, KH, ["sc2"])
                tt("dve", ts1, ts1, ts2, ALU.add, ["sc1", "sc2"], ["sc1"])
                tt("dve", cur, cur, ts1, ALU.add, ["H", "sc1"], ["H"])
            for i in (range(7) if fill else ()):
                pv, vi = blkv(7, 0, nv), blkv(i, 1, nv)
                ar_ = pat(PAr[P_, i, 0:1], [[0, nv], [0, 2], [1, 32]])
                ai_ = pat(AIp[P_, i, 0, 0:1], [[0, nv], [32, 2], [1, 32]])
                tt("dve", t1v, pv, ar_, ALU.mult, KH, ["sc1"])
                tt("dve", t2v, rev_ap(pv, 2), ai_, ALU.mult, KH, ["sc2"])
                tt("dve", t1v, t1v, t2v, ALU.add, ["sc1", "sc2"], ["sc1"])
                tt("dve", vi, vi, t1v, ALU.add, ["H", "sc1"], ["H"])

        sb_scan = [carve(138, [128, 2, 2, 32]), carve(139, [128, 2, 2, 32])]

        sb_kd = [carve(70, [128, 512], BF16), carve(71, [128, 512], BF16)]
        PMB = [carve(72, [128, 4, 128], BF16), carve(73, [128, 4, 128], BF16)]
        QFB = [carve(74, [128, 4, 128], BF16), carve(75, [128, 4, 128], BF16)]
        QBB = [carve(76, [128, 4, 128], BF16), carve(77, [128, 4, 128], BF16)]
        ONB = [carve(78, [128, 512]), carve(80, [128, 512])]
        OMB = [carve(82, [128, 512], BF16), carve(83, [128, 512], BF16)]
        SIGB = [carve(84, [128, 512], BF16), carve(85, [128, 512], BF16)]
        FINT = [carve(86, [32, 256]), carve(87, [32, 256])]
        FIN = sb("FIN", [128, 2, 32])
        DBGY = carve(90, [128, 4096])
        hT = carve(138, [128, 8, 1024], BF16)
        stage1a(xs[1024:2048, :], 1024, 1, hT, 0, 1)
        if "stop_o1" in dbg:
            dbg_out["hT"] = (hT[:, 0, :], [128, 1024], ["hT"], BF16)
            barrier()
            return
        for h in range(2):
            pass
        dma("sp", Sst[1][:], sret[1].rearrange("h d e -> d h e"), G["st1"], BARK, ["Sst1"])
        dma("sp", Sst[0][:], sret[0].rearrange("h d e -> d h e"), G["st0"], BARK, ["Sst0"])
        barrier()
        u_cm = carve(186, [128, 32, 8, 16], BF16)
        kdt = [carve(154, [128, 512], BF16), carve(155, [128, 512], BF16)]
        vtt = [carve(156, [128, 512], BF16), carve(157, [128, 512], BF16)]
        for t in range(7, -1, -1):
            b_ = t % 2
            bank, bk = proj_tm(hT, t, 512)
            tt("dve", kdt[b_], bank.rearrange("p (h e) -> p h e", h=4), kdb_tab, ALU.mult, [bk, "rtab"], ["kdt%d" % b_]) if False else \
                tt("dve", kdt[b_].rearrange("p (h e) -> p h e", h=4), bank.rearrange("p (h e) -> p h e", h=4), kdb_tab, ALU.mult,
                   [bk, "rtab"], ["kdt%d" % b_])
            bank2, bk2 = proj_tm(hT, t, 1024)
            cp("act", vtt[b_], bank2, [bk2], ["vtt%d" % b_])
            kv_update(1, kdt[b_], vtt[b_], ["kdt%d" % b_, "vtt%d" % b_])
        proj_u(hT, 1024, u_cm)
        barrier()
        if "stop_o2" in dbg:
            dbg_out["Sst1"] = (Sst[1][:].rearrange("p h e -> p (h e)"), [128, 512], ["Sst1"], F32)
            dbg_out["ucm"] = (u_cm.rearrange("p g m c -> p (g m c)"), [128, 4096], ["u_cm"], BF16)
            return
        H = carve(162, [128, 1, 128, 2, 32])
        u_arr = carve(154, [128, 32, 128], BF16)
        s5_sums(u_cm, 128, 1, H, u_arr)
        barrier()
        if "stop_o3" in dbg:
            dbg_out["H"] = (H[:, 0, 0:16].rearrange("p k r g -> p (k r g)"), [128, 1024], ["H"], F32)
            return
        s5_scan_blk(H, 128, Z0[:].rearrange("p (s r) g -> p s r g", s=1), lo=64, fill=False)
        cp("dve", Hmid[64:128], H[64:128, 0, 127], ["H"], ["Hmid"])
        barrier()
        if "stop_o4" in dbg:
            dbg_out["Hmid"] = (Hmid[:].rearrange("p r g -> p (r g)"), [128, 64], ["Hmid"], F32)
            dbg_out["H"] = (H[:, 0, 0:16].rearrange("p k r g -> p (k r g)"), [128, 1024], ["H"], F32)
            return

        X = carve(138, [128, 12, 1024])
        segs = [dict(name="S", xsrc=xs[0:1024, :], ntok=1024, ci=1, nseq=1, moff=0),
                dict(name="P", xsrc=xp, ntok=512, ci=0, nseq=2, moff=1024)]
        for sg_ in segs:
            ntok, ci, nseq, moff = sg_["ntok"], sg_["ci"], sg_["nseq"], sg_["moff"]
            isS = sg_["name"] == "S"
            nt = ntok // 128
            nts = nt // nseq
            nch = ntok // 8
            nk = nch // nseq
            hT = carve(138, [128, 8, ntok], BF16)
            if not isS:
                for k2 in range(4):
                    dma("pool", w_in_bf[:, 2 * k2:2 * k2 + 2, :], w_in_v[:, 2 * k2:2 * k2 + 2, :], d_win2, BARK, ["w_in"])
            stage1a(sg_["xsrc"], ntok, ci, hT, 0, 1)
            barrier()
            qT = carve(154, [128, 4, ntok], BF16)
            kT = carve(162, [128, 4, ntok], BF16)
            vv = carve(170, [128, nt, 512], BF16)
            sgt = carve(178, [128, nt, 512], BF16)
            u_cm = carve(186, [128, 32, 8, 16], BF16)
            for tg in range(ntok // 512):
                for h in range(4):
                    proj_fm(hT, tg, h * 128, qT[:, h, tg * 512:(tg + 1) * 512], h % 2)
                    proj_fm(hT, tg, 512 + h * 128, kT[:, h, tg * 512:(tg + 1) * 512], (h + 1) % 2)
            for t in range(nt):
                bank, bk = proj_tm(hT, t, 1024)
                cp("dve", vv[:, t, :], bank, [bk], ["vv"])
                bank, bk = proj_tm(hT, t, 1536)
                act(sgt[:, t, :], bank, AF.Silu, [bk], ["sgt"])
            proj_u(hT, ntok, u_cm)
            barrier()
            Sbf = carve(138, [128, 2, nt, 4, 128], BF16)
            kdt = [carve(194 - 2, [128, 512], BF16), carve(194 - 1, [128, 512], BF16)] if False else \
                [sb_kd[0][:], sb_kd[1][:]]
            for sq in range(nseq):
                if not isS:
                    memset("dve", Sst[0][:], 0.0, ["Sst0"])
                    memset("pool", Sst[1][:], 0.0, ["Sst1"])
                for idx in range(nts):
                    for d_ in range(2):
                        tl = idx if d_ == 0 else nts - 1 - idx
                        t = sq * nts + tl
                        cp("act", Sbf[:, d_, t], Sst[d_][:], ["Sst%d" % d_], ["Sbf%d" % d_])
                        for h in range(4):
                            tr(psT[d_][:, h * 128:(h + 1) * 128], kT[:, h, t * 128:(t + 1) * 128], ident_b[:], ["proj"], ["psT%d" % d_])
                        tab = kdf_tab if d_ == 0 else kdb_tab
                        tt("dve", kdt[d_].rearrange("p (h e) -> p h e", h=4), psT[d_][:, 0:512].rearrange("p (h e) -> p h e", h=4),
                           tab, ALU.mult, ["psT%d" % d_, "rtab"], ["kdt%d" % d_])
                        kv_update(d_, kdt[d_], vv[:, t, :], ["kdt%d" % d_, "vv"])
                if not isS:
                    for d_ in range(2):
                        dma("sp", nret[sq, d_].rearrange("h d e -> d h e"), Sst[d_][:], d_out, ["Sst%d" % d_], [])
            barrier()
            ST5 = [(carve(88 + 0.25 * k_, [128, 4, 6]), carve(88.125 + 0.25 * k_, [128, 4, 2]),
                    carve(88.1875 + 0.25 * k_, [128, 4]), carve(88.21875 + 0.25 * k_, [128, 4])) for k_ in range(2)]
            for t in range(nt):
                bank, bk = halfbank()
                for h in range(4):
                    mm(bank[:, h * 128:(h + 1) * 128], kT[:, h, t * 128:(t + 1) * 128], qT[:, h, t * 128:(t + 1) * 128], True, True,
                       ["proj"], [bk])
                b_ = t % 2
                Pm, qf, qb, on, om = PMB[b_], QFB[b_], QBB[b_], ONB[b_], OMB[b_]
                tt("dve", Pm[:], bank.rearrange("p (h i) -> p h i", h=4), maskT, ALU.mult, [bk, "rtab"], ["Pm%d" % b_])
                tt("pool", qf[:], qT[:, :, t * 128:(t + 1) * 128], colf, ALU.mult, ["proj", "rtab"], ["qf%d" % b_])
                tt("pool", qb[:], qT[:, :, t * 128:(t + 1) * 128], colb, ALU.mult, ["proj", "rtab"], ["qb%d" % b_])
                po, pk = halfbank()
                for h in range(4):
                    o_ap = po[:, h * 128:(h + 1) * 128]
                    mm(o_ap, Pm[:, h, :], vv[:, t, h * 128:(h + 1) * 128], True, False, ["Pm%d" % b_, "vv"], [pk])
                    mm(o_ap, qf[:, h, :], Sbf[:, 0, t, h, :], False, False, ["qf%d" % b_, "Sbf"], [pk])
                    mm(o_ap, qb[:, h, :], Sbf[:, 1, t, h, :], False, True, ["qb%d" % b_, "Sbf"], [pk])
                st6_, mv_, rs_, nm_ = ST5[b_]
                k5 = "s5st%d" % b_
                for h in range(4):
                    S.op("dve", lambda e, h=h, po=po, st6_=st6_: e.bn_stats(out=st6_[:, h, :], in_=po[:, h * 128:(h + 1) * 128]), [pk], [k5])
                for h in range(4):
                    S.op("dve", lambda e, h=h, st6_=st6_, mv_=mv_: e.bn_aggr(out=mv_[:, h, :], in_=st6_[:, h:h + 1, :]), [k5], [k5])
                act(rs_, mv_[:, :, 1], AF.Ln, [k5], [k5], bias=epsc[:, 0:1])
                act(rs_, rs_, AF.Exp, [k5], [k5], scale=-0.5)
                stt("dve", nm_, mv_[:, :, 0], -1.0, rs_, ALU.mult, ALU.mult, [k5], [k5])
                for h in range(4):
                    act(on[:, h * 128:(h + 1) * 128], po[:, h * 128:(h + 1) * 128], AF.Identity, [pk, k5], ["on%d" % b_],
                        bias=nm_[:, h:h + 1], scale=rs_[:, h:h + 1])
                tt("pool", om[:], on[:], sgt[:, t, :], ALU.mult, ["on%d" % b_, "sgt"], ["om%d" % b_])
                for h in range(4):
                    tr(psT[1][:, h * 128:(h + 1) * 128], om[:, h * 128:(h + 1) * 128], ident_b[:], ["om%d" % b_], ["psT1"])
                cp("act", mixT[:, 0:4, moff + t * 128:moff + (t + 1) * 128], psT[1][:, 0:512].rearrange("p (h i) -> p h i", h=4),
                   ["psT1"], ["mixT"])
            barrier()
            H = carve(162, [128, nseq, nk, 2, 32])
            u_arr = carve(154, [128, 32, nch], BF16)
            s5_sums(u_cm, nch, nseq, H, u_arr)
            barrier()
            init = Hmid[:].rearrange("p (s r) g -> p s r g", s=1) if isS else zero64[:]
            if isS:
                s5_scan_blk(H, nk, init)
            else:
                s5_scan(H, nseq, nk, init)
            barrier()
            Hb = carve(138, [128, 2, 32, nch], BF16)
            for sq in range(nseq):
                j0 = sq * nk
                for lo, hi in ((0, 64), (64, 128)):
                    fwd = lo == 0
                    jinit = j0 if fwd else j0 + nk - 1
                    cp("dve", Hb[lo:hi, :, :, jinit], init[lo:hi, sq], ["init"], ["Hb"])
                    src = pat(H[lo:hi, sq, 0, 0, 0:1], [[32, 2], [1, 32], [64, nk - 1]])
                    if fwd:
                        cp("dve", Hb[lo:hi, :, :, j0 + 1:j0 + nk], src, ["H"], ["Hb"])
                    else:
                        cp("act", Hb[lo:hi, :, :, j0:j0 + nk - 1], rev_ap(src, 3), ["H"], ["Hb"])
            if not isS:
                for sq in range(nseq):
                    fin = FIN
                    last = H[:, sq, nk - 1]
                    tt("dve", fin[:, 0, :], last[:, 0, :], sm["p7r"][:], ALU.mult, ["H"], ["fin"])
                    tt("dve", sm["t1"][:], last[:, 1, :], sm["p7i"][:], ALU.mult, ["H", "fin"], ["sm_t1"])
                    tt("dve", fin[:, 0, :], fin[:, 0, :], sm["t1"][:], ALU.subtract, ["fin", "sm_t1"], ["fin"])
                    tt("dve", fin[:, 1, :], last[:, 0, :], sm["p7i"][:], ALU.mult, ["H"], ["fin"])
                    tt("dve", sm["t1"][:], last[:, 1, :], sm["p7r"][:], ALU.mult, ["H", "fin"], ["sm_t1"])
                    tt("dve", fin[:, 1, :], fin[:, 1, :], sm["t1"][:], ALU.add, ["fin", "sm_t1"], ["fin"])
                    bank, bk = halfbank()
                    for ri in range(2):
                        mm(bank[0:32, ri * 128:(ri + 1) * 128], fin[:, ri, :], ident_f[:], True, True, ["fin"], [bk])
                    cp("dve", FINT[sq][:], bank[0:32, 0:256], [bk], ["fint%d" % sq])
                    for ri, dst in enumerate((ns5r, ns5i)):
                        dma("sp", dst[sq].rearrange("d g p -> g d p"), FINT[sq][:, ri * 128:(ri + 1) * 128].rearrange("g (d p) -> g d p", d=2),
                            d_out, ["fint%d" % sq], [])
            y2cm = carve(162, [128, 8, 32, 16], BF16)
            y2T = carve(170, [128, 4, ntok], BF16)
            barrier()
            for g8 in range(4):
                po = psD[g8 % 3]
                pk = "psy%d" % (g8 % 3)
                for gi in range(8):
                    g_ = g8 * 8 + gi
                    o_ap = po[0:nch, gi * 128:(gi + 1) * 128]
                    mm(o_ap, u_arr[:, g_, :], Mtoe[:, g_, :], True, False, ["u_arr", "Mtoe"], [pk])
                    mm(o_ap, Hb[:, 0, g_, :], CoR[:, g_, :], False, False, ["Hb", "CoR"], [pk])
                    mm(o_ap, Hb[:, 1, g_, :], CoI[:, g_, :], False, True, ["Hb", "CoI"], [pk])
                act(y2cm[0:nch, :, g8 * 8:(g8 + 1) * 8, :].rearrange("p t g c -> p g t c"), po[0:nch, :].rearrange("p (g t c) -> p g t c", g=8, t=8),
                    AF.Gelu_apprx_tanh, [pk], ["y2cm"])
                if "s5y" in dbg and not isS:
                    cp("dve", DBGY[0:nch, g8 * 1024:(g8 + 1) * 1024], po[0:nch, :], [pk], ["dbgy"])
            for fb in range(4):
                b_ = fb % 2
                for t8 in range(8):
                    src = y2cm[0:nch, t8, fb * 8:(fb + 1) * 8, :].rearrange("p g c -> p (g c)")
                    tr(psT[b_][:, t8 * 128:t8 * 128 + nch], src, ident_b[0:nch, 0:nch], ["y2cm"], ["psT%d" % b_])
                o_ap = pat(y2T[:, fb, 0:1], [[8, nch], [1, 8]])
                i_ap = pat(psT[b_][:, 0:1], [[1, nch], [128, 8]])
                cp("act" if fb % 2 else "dve", o_ap, i_ap, ["psT%d" % b_], ["y2T"])
            for n_ in range(4):
                for tg in range(ntok // 512):
                    bank, bk = halfbank()
                    for fb in range(4):
                        mm(bank, w_glu_bf[:, fb, n_ * 128:(n_ + 1) * 128], y2T[:, fb, tg * 512:(tg + 1) * 512], fb == 0, fb == 3,
                           ["w_glu", "y2T"], [bk])
                    b_ = (n_ + tg) % 2
                    act(SIGB[b_][:], bank, AF.Sigmoid, [bk, "bgluT"], ["sig%d" % b_], bias=bgluT[:, n_:n_ + 1])
                    tt("pool", mixT[:, 4 + n_, moff + tg * 512:moff + (tg + 1) * 512], y2T[:, n_, tg * 512:(tg + 1) * 512], SIGB[b_][:],
                       ALU.mult, ["y2T", "sig%d" % b_], ["mixT"])
            barrier()
        if "s5y" in dbg:
            dbg_out["y"] = (DBGY[0:64, :], [64, 4096], ["dbgy"], F32)
        if "mix" in dbg:
            dbg_out["mixT"] = (mixT[:, :, 1024:1536].rearrange("p k n -> p (k n)"), [128, 4096], ["mixT"], BF16)

        w_out_bf = carve(70, [128, 8, 1024], BF16)
        ln1g_bc = carve(86, [128, 1024])
        ln1b_bc = carve(90, [128, 1024])
        XIN2 = [carve(94 + 4 * k_, [128, 1024]) for k_ in range(4)]
        RT = [carve(32 + 4 * k_, [128, 1024]) for k_ in range(4)]
        C2ST = [(carve(48 + 0.25 * k_, [128, 4, 6]), carve(48.125 + 0.25 * k_, [128, 4, 2]),
                 carve(48.1875 + 0.25 * k_, [128, 4]), carve(48.21875 + 0.25 * k_, [128, 4])) for k_ in range(4)]
        d_x4 = d_x + [S.dsem("x2"), S.dsem("x3")]
        d_wo = S.dsem("wo")
        dma("pool", w_out_bf, w_out.rearrange("(kc p) n -> p kc n", p=128), d_wo, BARK, ["w_out"])
        dma("sp", ln1g_bc, rowbc(ln1_g, 128), d_ld2, BARK, ["ln1g"])
        dma("sp", ln1b_bc, rowbc(ln1_b, 128), d_ld2, BARK, ["ln1b"])
        W1s = [carve(0, [128, 8, 1024], BF16), carve(32, [128, 8, 1024], BF16)]
        W2s = [carve(16, [128, 8, 1024], BF16), carve(70, [128, 8, 1024], BF16)]
        d_ffa = [S.dsem("ffa0"), S.dsem("ffa1")]
        d_ffb = [S.dsem("ffb0"), S.dsem("ffb1")]
        w_ff1_v = w_ff1.rearrange("(kc p) n -> p kc n", p=128)
        w_ff2_v = w_ff2.rearrange("(q fc p) n -> q p fc n", q=4, p=128)

        def load_ff(q_):
            s_ = q_ % 2
            dma("pool", W1s[s_], w_ff1_v[:, :, q_ * 1024:(q_ + 1) * 1024], d_ffa[s_], BARK, ["W1_%d" % s_])
            dma("pool", W2s[s_], w_ff2_v[q_], d_ffb[s_], BARK, ["W2_%d" % s_])

        load_ff(0)

        def own_x(t):
            return (xs[t * 128:(t + 1) * 128, :], 1) if t < 8 else (xp[(t - 8) * 128:(t - 7) * 128, :], 0)

        for t in range(12):
            b_ = t % 4
            src, ci = own_x(t)
            dma("sp", XIN2[b_], src, d_x4[b_], BARK, ["xin2_%d" % b_])
            po = psD[t % 3][:]
            pk = "pso%d" % (t % 3)
            for hh in range(2):
                for kc in range(8):
                    mm(po[:, hh * 512:(hh + 1) * 512], mixT[:, kc, t * 128:(t + 1) * 128], w_out_bf[:, kc, hh * 512:(hh + 1) * 512],
                       kc == 0, kc == 7, ["mixT", "w_out"], [pk])
            r_ = RT[b_]
            tt("dve", r_, po, g1bc[:, ci, :], ALU.mult, [pk, "gbc"], ["rt%d" % b_])
            stt("dve", r_, XIN2[b_], ALPHA, r_, ALU.mult, ALU.add, ["xin2_%d" % b_, "rt%d" % b_], ["rt%d" % b_])
            tl_ = C2ST[b_]
            kst = "c2st%d" % b_
            ln_stats(r_, 2, 512, ["rt%d" % b_], col=0, tiles=tl_, tag=kst)
            act(X[:, t, :], r_, AF.Identity, ["rt%d" % b_, kst], ["X%d" % t], bias=tl_[3][:, 0:1], scale=tl_[2][:, 0:1])
            tt("pool", X[:, t, :], X[:, t, :], ln1g_bc, ALU.mult, ["X%d" % t, "ln1g"], ["X%d" % t])
            tt("pool", X[:, t, :], X[:, t, :], ln1b_bc, ALU.add, ["X%d" % t, "ln1b"], ["X%d" % t])
        if "x1" in dbg:
            dbg_out["x1"] = (X[:, 8, :], [128, 1024], ["X8"], F32)
        barrier()

        load_ff(1)
        h2T = carve(114, [128, 8, 1536], BF16)
        ln2g_bc = carve(102, [128, 1024])
        ln2b_bc = carve(106, [128, 1024])
        gb2 = carve(186, [128, 2, 1024])
        TT = [carve(48, [128, 1024]), carve(110, [128, 1024])]
        FT = [carve(86, [128, 8, 512], BF16), carve(94, [128, 8, 512], BF16)]
        bff1T = sb("bff1T", [128, 32])
        RELB = [carve(52, [128, 512], BF16), carve(53, [128, 512], BF16)]
        dma("sp", ln2g_bc, rowbc(ln2_g, 128), d_ld3, BARK, ["ln2g"])
        dma("sp", ln2b_bc, rowbc(ln2_b, 128), d_ld3, BARK, ["ln2b"])
        dma("sp", bff1T[:], b_ff1.rearrange("(c p) -> p c", p=128), G["ld4"], BARK, ["bff1T"], noncontig=True)
        for ci in range(2):
            dma("sp", gb2[:, ci, :], rowbc(b_ff2, 128), G["ld5"], BARK, ["gb2"])
        tt("dve", gb2, gb2, g2bc, ALU.mult, ["gb2", "gbc"], ["gb2"])
        XN2 = [carve(98, [128, 1024], BF16), carve(100, [128, 1024], BF16)]
        EV32D = [carve(86, [128, 8, 128]), carve(90, [128, 8, 128])]
        for t in range(12):
            b_ = t % 2
            ci = 1 if t < 8 else 0
            ln_stats(X[:, t, :], 2, 512, ["X%d" % t], col=b_)
            act(XN2[b_], X[:, t, :], AF.Identity, ["X%d" % t, "rstd_%d" % b_, "nmr_%d" % b_], ["xn%d" % b_], bias=nmr[:, b_:b_ + 1],
                scale=rstd[:, b_:b_ + 1])
            for c in range(8):
                tr(psT[b_][:, c * 128:(c + 1) * 128], XN2[b_][:, c * 128:(c + 1) * 128], ident_b[:], ["xn%d" % b_], ["psT%d" % b_])
            tmp32 = EV32D[b_]
            sc_bc = pat(modT[:, 3, 0, ci:ci + 1], [[2, 8], [0, 128]])
            sh_bc = pat(modT[:, 2, 0, ci:ci + 1], [[2, 8], [0, 128]])
            tt("dve", tmp32, psT[b_][:].rearrange("p (c n) -> p c n", c=8), sc_bc, ALU.mult, ["psT%d" % b_, "modT"], ["ev32_%d" % b_])
            tt("pool", h2T[:, :, t * 128:(t + 1) * 128], tmp32, sh_bc, ALU.add, ["ev32_%d" % b_, "modT"], ["h2T"])
            stt("dve", X[:, t, :], X[:, t, :], ALPHA, gb2[:, ci, :], ALU.mult, ALU.add, ["X%d" % t, "gb2"], ["X%d" % t])
        ffb = [psT[0][:].bitcast(F32), psT[1][:].bitcast(F32), psD[0][:, 0:512], psD[0][:, 512:1024]]
        ffk = ["psT0", "psT1", "psD0_0", "psD0_1"]
        ffi = [0]
        for q_ in range(4):
            s_ = q_ % 2
            if 1 <= q_ <= 2:
                load_ff(q_ + 1)
            for tg in range(3):
                fb_ = (q_ * 3 + tg) % 2
                ft = FT[fb_]
                for fc in range(8):
                    ffi[0] = (ffi[0] + 1) % 4
                    bank, bk = ffb[ffi[0]], ffk[ffi[0]]
                    for kc in range(8):
                        mm(bank, W1s[s_][:, kc, fc * 128:(fc + 1) * 128], h2T[:, kc, tg * 512:(tg + 1) * 512], kc == 0, kc == 7,
                           ["W1_%d" % s_, "h2T"], [bk])
                    act(RELB[fc % 2], bank, AF.Relu, [bk, "bff1T"], ["relb%d" % (fc % 2)],
                        bias=bff1T[:, q_ * 8 + fc:q_ * 8 + fc + 1])
                    tt("pool", ft[:, fc, :], RELB[fc % 2], RELB[fc % 2], ALU.mult, ["relb%d" % (fc % 2)], ["ft%d_%d" % (fb_, fc)])
                for tl in range(4):
                    t = tg * 4 + tl
                    ci = 1 if t < 8 else 0
                    po = psD[1 + (t % 2)][:]
                    pk = "pso%d" % (1 + (t % 2))
                    for hh in range(2):
                        for fc in range(8):
                            mm(po[:, hh * 512:(hh + 1) * 512], ft[:, fc, tl * 128:(tl + 1) * 128], W2s[s_][:, fc, hh * 512:(hh + 1) * 512],
                               fc == 0, fc == 7, ["ft%d_%d" % (fb_, fc), "W2_%d" % s_], [pk])
                    tb_ = TT[t % 2]
                    tt("dve", tb_, po, g2bc[:, ci, :], ALU.mult, [pk, "gbc"], ["tt%d" % (t % 2)])
                    tt("pool", X[:, t, :], X[:, t, :], tb_, ALU.add, ["X%d" % t, "tt%d" % (t % 2)], ["X%d" % t])
        for t in range(12):
            c2_ = t % 2
            ln_stats(X[:, t, :], 2, 512, ["X%d" % t], col=c2_)
            tb_ = TT[t % 2]
            act(tb_, X[:, t, :], AF.Identity, ["X%d" % t, "rstd_%d" % c2_, "nmr_%d" % c2_], ["tt%d" % (t % 2)], bias=nmr[:, c2_:c2_ + 1],
                scale=rstd[:, c2_:c2_ + 1])
            tt("pool", tb_, tb_, ln2g_bc, ALU.mult, ["tt%d" % (t % 2), "ln2g"], ["tt%d" % (t % 2)])
            tt("pool", tb_, tb_, ln2b_bc, ALU.add, ["tt%d" % (t % 2), "ln2b"], ["tt%d" % (t % 2)])
            dst = ys[t * 128:(t + 1) * 128, :] if t < 8 else yp[(t - 8) * 128:(t - 7) * 128, :]
            dma("sp", dst, tb_, d_oy[t % 2], ["tt%d" % (t % 2)], [])


    if "stop_setup" not in dbg:
        main_phases()

    return finish()


def make_in_maps(inp):
    g = lambda k: np.ascontiguousarray(np.asarray(inp[k], dtype=np.float32))
    x_prompt, x_sample = g("x_prompt"), g("x_sample")
    state_ret, s5r, s5i = g("state_ret"), g("state_s5_re"), g("state_s5_im")
    c, c_ctx = g("c"), g("c_ctx")
    maps = []
    for core in range(8):
        b, half = core // 2, core % 2
        flip = half == 1
        dsl = slice(None, None, -1) if flip else slice(None)
        xs_full = x_sample[b]
        xs_v = xs_full[::-1] if flip else xs_full
        xp_v = x_prompt[2 * core:2 * core + 2]
        if flip:
            xp_v = xp_v[:, ::-1]
        m = {
            "xs": np.ascontiguousarray(xs_v),
            "xp": np.ascontiguousarray(xp_v.reshape(512, 1024)),
            "cond": np.ascontiguousarray(np.stack([c_ctx, c[b]])),
            "sret": np.ascontiguousarray(state_ret[b, 0][dsl]),
            "ss5r": np.ascontiguousarray(s5r[b, 0][dsl]),
            "ss5i": np.ascontiguousarray(s5i[b, 0][dsl]),
            "w_ada": g("w_ada")[0], "b_ada": g("b_ada")[0].reshape(1, 6144), "w_in": g("w_in")[0],
            "ret_decay": np.ascontiguousarray(g("ret_decay")[0][dsl]).reshape(1, 8),
            "a_re": np.ascontiguousarray(g("s5_a_re")[0][dsl]),
            "a_im": np.ascontiguousarray(g("s5_a_im")[0][dsl]),
            "log_dt": np.ascontiguousarray(g("s5_log_dt")[0][dsl]),
            "b_re": g("s5_b_re")[0], "b_im": g("s5_b_im")[0], "c_re": g("s5_c_re")[0], "c_im": g("s5_c_im")[0],
            "s5_d": g("s5_d")[0].reshape(1, 512), "w_glu": g("w_glu")[0], "b_glu": g("b_glu")[0],
            "w_out": g("w_out")[0], "ln1_g": g("ln1_g")[0].reshape(1, 1024), "ln1_b": g("ln1_b")[0].reshape(1, 1024),
            "w_ff1": g("w_ff1")[0], "b_ff1": g("b_ff1")[0], "w_ff2": g("w_ff2")[0],
            "b_ff2": g("b_ff2")[0].reshape(1, 1024),
            "ln2_g": g("ln2_g")[0].reshape(1, 1024), "ln2_b": g("ln2_b")[0].reshape(1, 1024),
        }
        maps.append(m)
    return maps


def kernel(**inp):
    nc, _ = build()
    maps = make_in_maps(inp)
    res = run_bass_kernel_spmd(nc, maps, core_ids=list(range(8)))
    y_p = np.zeros((16, 256, 1024), np.float32)
    y_s = np.zeros((4, 2048, 1024), np.float32)
    n_ret = np.zeros((16, 1, 2, 4, 128, 128), np.float32)
    n_s5r = np.zeros((16, 1, 2, 32, 64), np.float32)
    n_s5i = np.zeros((16, 1, 2, 32, 64), np.float32)
    for core in range(8):
        r = res.results[core]
        b, half = core // 2, core % 2
        flip = half == 1
        dsl = slice(None, None, -1) if flip else slice(None)
        ypc = r["yp"].reshape(2, 256, 1024)
        ysc = r["ys"]
        if flip:
            ypc = ypc[:, ::-1]
            y_s[b, 1024:] = ysc[::-1]
        else:
            y_s[b, :1024] = ysc
        y_p[2 * core:2 * core + 2] = ypc
        n_ret[2 * core:2 * core + 2, 0] = r["nret"][:, dsl]
        n_s5r[2 * core:2 * core + 2, 0] = r["ns5r"][:, dsl]
        n_s5i[2 * core:2 * core + 2, 0] = r["ns5i"][:, dsl]
    return (y_p, y_s, n_ret, n_s5r, n_s5i)
```
